# Optimizing a Trainium2 kernel written in Bass

```python
import math
import jax, jax.numpy as jnp
from jax import lax
import numpy as np

D_MODEL = 2048
BATCH = 32
SEQ = 256
DEPTH = 2
DEC_BATCH = 8
DEC_SEQ = 4096
PAST_LEN = 256

GRID_W = 64
ATT_WIDTH = D_MODEL // 2
HY_WIDTH = D_MODEL // 4
POOL_WIDTH = D_MODEL - ATT_WIDTH - HY_WIDTH
ATT_HD = 64
ATT_VD = 2 * ATT_HD
N_ATT_HEADS = ATT_WIDTH // ATT_VD
Q_BLOCK = 128
ROPE_THETA = 10000.0
HY_ORDER = 2
HY_SHORT = 3
HY_BANDS = 16
HY_EMB = 1 + 2 * HY_BANDS
HY_FILTER_HIDDEN = 64
HY_DECAY_TARGET = 1e-2
HY_FAST_DECAY = 0.3
HY_SLOW_DECAY = 1.5
HY_DECAY_SHIFT = 0.05
HY_MIN_DECAY = math.log(HY_DECAY_TARGET) / HY_SLOW_DECAY
HY_MAX_DECAY = math.log(HY_DECAY_TARGET) / HY_FAST_DECAY
POOL_WINDOWS = (2, 4, 8, 16)
N_POOL_GROUPS = len(POOL_WINDOWS)
POOL_GW = POOL_WIDTH // N_POOL_GROUPS
FFN_HIDDEN = ((8 * D_MODEL + 3 * 256 - 1) // (3 * 256)) * 256
IN_COLS = 3 * ATT_WIDTH + (HY_ORDER + 1) * HY_WIDTH + POOL_WIDTH
IN_SPLITS = (ATT_WIDTH, 2 * ATT_WIDTH, 3 * ATT_WIDTH, 3 * ATT_WIDTH + (HY_ORDER + 1) * HY_WIDTH)
EPS = 1e-6
F32 = jnp.float32

kernel_name = "hybrid_diffattn_hyena_pool_dit_step"


def rms_norm(x, g):
    xf = x.astype(F32)
    y = xf * lax.rsqrt(jnp.mean(xf * xf, axis=-1, keepdims=True) + EPS)
    return (y * g.astype(F32)).astype(x.dtype)


def axial_rope_tables(n_tokens):
    n_rows = n_tokens // GRID_W
    rows = jnp.repeat(jnp.arange(n_rows), GRID_W).astype(F32)
    cols = jnp.tile(jnp.arange(GRID_W), n_rows).astype(F32)
    quarter = ATT_HD // 4
    inv = ROPE_THETA ** (-jnp.arange(quarter, dtype=F32) / quarter)
    ang = jnp.concatenate([rows[:, None] * inv, cols[:, None] * inv], -1)
    return jnp.cos(ang)[None, :, None, None, :], jnp.sin(ang)[None, :, None, None, :]


def apply_rope(x, cos, sin):
    half = ATT_HD // 2
    x1, x2 = x[..., :half].astype(F32), x[..., half:].astype(F32)
    return jnp.concatenate([x1 * cos - x2 * sin, x2 * cos + x1 * sin], -1).astype(x.dtype)


def diff_attention(q, k, v, lam):
    B, Lq = q.shape[0], q.shape[1]
    nb = Lq // Q_BLOCK
    qb = jnp.moveaxis(q.reshape(B, nb, Q_BLOCK, N_ATT_HEADS, 2, ATT_HD), 1, 0)
    scale = ATT_HD ** -0.5

    def block(qi):
        s = jnp.einsum('bqhjd,bkhjd->bhjqk', qi, k).astype(F32) * scale
        p = jax.nn.softmax(s, axis=-1)
        a = p[:, :, 0] - lam * p[:, :, 1]
        return jnp.einsum('bhqk,bkhe->bqhe', a.astype(v.dtype), v)

    out = lax.map(block, qb)
    return jnp.moveaxis(out, 0, 1).reshape(B, Lq, N_ATT_HEADS, ATT_VD)


def short_conv(u, w, b):
    L = u.shape[1]
    pad = HY_SHORT // 2
    up = jnp.pad(u, ((0, 0), (pad, pad), (0, 0)))
    return sum(up[:, j:j + L] * w[j] for j in range(HY_SHORT)) + b


def hyena_filter_spectrum(L, w1, b1, fr1, w2, b2, fr2, w3):
    t = jnp.linspace(0.0, 1.0, L, dtype=F32)[:, None]
    wpos = 2.0 * math.pi * jnp.arange(L, dtype=F32)[:, None] / L
    bands = jnp.linspace(1e-4, HY_BANDS - 1, HY_BANDS, dtype=F32)[None, :]
    z = jnp.concatenate([t, jnp.cos(bands * wpos), jnp.sin(bands * wpos)], -1)
    hdn = jnp.sin(fr1.astype(F32) * (z @ w1.astype(F32) + b1.astype(F32)))
    hdn = jnp.sin(fr2.astype(F32) * (hdn @ w2.astype(F32) + b2.astype(F32)))
    filt = (hdn @ w3.astype(F32)).reshape(L, 2, HY_ORDER, HY_WIDTH)
    deltas = jnp.abs(jnp.linspace(HY_MIN_DECAY, HY_MAX_DECAY, HY_WIDTH, dtype=F32))
    window = jnp.exp(-t * deltas) + HY_DECAY_SHIFT
    filt = filt * window[:, None, None, :]
    fwd, bwd = filt[:, 0], filt[:, 1]
    k_circ = jnp.concatenate([fwd, jnp.zeros((1, HY_ORDER, HY_WIDTH), F32), bwd[1:][::-1]], 0)
    k_circ = k_circ / jnp.sum(jnp.abs(k_circ), axis=0, keepdims=True)
    return jnp.fft.rfft(k_circ, axis=0)


def hyena_mixer(u, w_sc, b_sc, w1, b1, fr1, w2, b2, fr2, w3, hy_d):
    L = u.shape[1]
    parts = jnp.split(short_conv(u, w_sc, b_sc), HY_ORDER + 1, axis=-1)
    kf = hyena_filter_spectrum(L, w1, b1, fr1, w2, b2, fr2, w3)
    z = parts[0]
    for o in range(HY_ORDER):
        zf = jnp.fft.rfft(z.astype(F32), n=2 * L, axis=1)
        y = jnp.fft.irfft(zf * kf[None, :, o], n=2 * L, axis=1)[:, :L]
        z = parts[o + 1] * (y.astype(z.dtype) + z * hy_d[o])
    return z


def multiscale_pool(u, w_pool, pool_scale):
    B, L, _ = u.shape
    ug = u.reshape(B, L, N_POOL_GROUPS, POOL_GW)
    cs = jnp.concatenate([jnp.zeros((B, 1, N_POOL_GROUPS, POOL_GW), F32),
                          jnp.cumsum(ug.astype(F32), axis=1)], axis=1)
    t = jnp.arange(L)
    means = []
    for g, w in enumerate(POOL_WINDOWS):
        lo = jnp.clip(t - w // 2, 0, L)
        hi = jnp.clip(t - w // 2 + w, 0, L)
        means.append((cs[:, hi, g] - cs[:, lo, g]) / (hi - lo).astype(F32)[None, :, None])
    pooled = jnp.stack(means, axis=2).astype(u.dtype) - ug
    out = jnp.einsum('blgc,gcd->blgd', pooled, w_pool)
    return out.reshape(B, L, POOL_WIDTH) * pool_scale


def trunk_layer(x, cond, rope, ctx_k, ctx_v, lam_init, p):
    B, L, _ = x.shape
    mod = (jax.nn.silu(cond) @ p['w_ada'] + p['b_ada'])[:, None, :]
    sh1, sc1, g1, sh2, sc2, g2 = jnp.split(mod, 6, axis=-1)
    h = rms_norm(x, p['norm1']) * (1 + sc1) + sh1
    q, k, v, u_hy, u_pool = jnp.split(h @ p['w_in'], IN_SPLITS, axis=-1)
    q = rms_norm(q.reshape(B, L, N_ATT_HEADS, 2, ATT_HD), p['q_norm'])
    k = rms_norm(k.reshape(B, L, N_ATT_HEADS, 2, ATT_HD), p['k_norm'])
    v = v.reshape(B, L, N_ATT_HEADS, ATT_VD)
    if rope is not None:
        q = apply_rope(q, *rope)
        k_att = apply_rope(k, *rope)
    else:
        k_att = k
    if ctx_k is None:
        keys, vals = k_att, v
    else:
        keys = jnp.concatenate([k_att, ctx_k.astype(k.dtype)], axis=1)
        vals = jnp.concatenate([v, ctx_v.astype(v.dtype)], axis=1)
    lw = p['w_lambda'].astype(F32)
    lam = jnp.exp(jnp.sum(lw[0] * lw[1])) - jnp.exp(jnp.sum(lw[2] * lw[3])) + lam_init
    att = diff_attention(q, keys, vals, lam)
    att = (rms_norm(att, p['subln']) * (1.0 - lam_init)).reshape(B, L, ATT_WIDTH)
    hy = hyena_mixer(u_hy, p['w_sc'], p['b_sc'], p['hy_w1'], p['hy_b1'], p['hy_fr1'],
                     p['hy_w2'], p['hy_b2'], p['hy_fr2'], p['hy_w3'], p['hy_d'])
    po = multiscale_pool(u_pool, p['w_pool'], p['pool_scale'])
    mix = jnp.concatenate([att, hy, po], axis=-1) @ p['w_out']
    x = x + g1 * mix
    h2 = rms_norm(x, p['norm2']) * (1 + sc2) + sh2
    f = (jax.nn.silu(h2 @ p['w_gate']) * (h2 @ p['w_up'])) @ p['w_down']
    x = x + g2 * f
    return x, k, v


def setup_inputs(seed: int = 0) -> dict:
    key = jax.random.key(seed)
    ks = iter(jax.random.split(key, 48))

    def nrm(shape, scale=1.0):
        return jax.random.normal(next(ks), shape, F32) * scale

    def gain(shape):
        return 1.0 + 0.05 * nrm(shape)

    D = D_MODEL
    return {
        'x_prompt': nrm((BATCH, SEQ, D)),
        'x_sample': nrm((DEC_BATCH, DEC_SEQ, D)),
        'cache_k': nrm((DEC_BATCH, DEPTH, PAST_LEN, N_ATT_HEADS, 2, ATT_HD)),
        'cache_v': nrm((DEC_BATCH, DEPTH, PAST_LEN, N_ATT_HEADS, ATT_VD)),
        'c': nrm((DEC_BATCH, D)),
        'c_ctx': nrm((D,)),
        'w_ada': nrm((DEPTH, D, 6 * D), 0.5 * D ** -0.5),
        'b_ada': nrm((DEPTH, 6 * D), 0.02),
        'norm1': gain((DEPTH, D)),
        'norm2': gain((DEPTH, D)),
        'w_in': nrm((DEPTH, D, IN_COLS), D ** -0.5),
        'q_norm': gain((DEPTH, ATT_HD)),
        'k_norm': gain((DEPTH, ATT_HD)),
        'w_lambda': nrm((DEPTH, 4, ATT_HD), 0.1),
        'subln': gain((DEPTH, ATT_VD)),
        'w_sc': nrm((DEPTH, HY_SHORT, (HY_ORDER + 1) * HY_WIDTH), HY_SHORT ** -0.5),
        'b_sc': nrm((DEPTH, (HY_ORDER + 1) * HY_WIDTH), 0.02),
        'hy_w1': nrm((DEPTH, HY_EMB, HY_FILTER_HIDDEN), HY_EMB ** -0.5),
        'hy_b1': nrm((DEPTH, HY_FILTER_HIDDEN), 0.1),
        'hy_fr1': gain((DEPTH, HY_FILTER_HIDDEN)),
        'hy_w2': nrm((DEPTH, HY_FILTER_HIDDEN, HY_FILTER_HIDDEN), HY_FILTER_HIDDEN ** -0.5),
        'hy_b2': nrm((DEPTH, HY_FILTER_HIDDEN), 0.1),
        'hy_fr2': gain((DEPTH, HY_FILTER_HIDDEN)),
        'hy_w3': nrm((DEPTH, HY_FILTER_HIDDEN, 2 * HY_ORDER * HY_WIDTH), HY_FILTER_HIDDEN ** -0.5),
        'hy_d': nrm((DEPTH, HY_ORDER, HY_WIDTH), 0.5),
        'w_pool': nrm((DEPTH, N_POOL_GROUPS, POOL_GW, POOL_GW), POOL_GW ** -0.5),
        'pool_scale': gain((DEPTH, POOL_WIDTH)),
        'w_out': nrm((DEPTH, D, D), D ** -0.5),
        'w_gate': nrm((DEPTH, D, FFN_HIDDEN), D ** -0.5),
        'w_up': nrm((DEPTH, D, FFN_HIDDEN), D ** -0.5),
        'w_down': nrm((DEPTH, FFN_HIDDEN, D), FFN_HIDDEN ** -0.5),
    }


def reference(x_prompt, x_sample, cache_k, cache_v, c, c_ctx, w_ada, b_ada, norm1, norm2, w_in,
              q_norm, k_norm, w_lambda, subln, w_sc, b_sc, hy_w1, hy_b1, hy_fr1, hy_w2, hy_b2,
              hy_fr2, hy_w3, hy_d, w_pool, pool_scale, w_out, w_gate, w_up, w_down):
    lat_rope = axial_rope_tables(x_sample.shape[1])
    ctx_cond = c_ctx[None, :]
    y_p, y_s = x_prompt, x_sample
    ks, vs = [], []
    for l in range(DEPTH):
        p = dict(w_ada=w_ada[l], b_ada=b_ada[l], norm1=norm1[l], norm2=norm2[l], w_in=w_in[l],
                 q_norm=q_norm[l], k_norm=k_norm[l], w_lambda=w_lambda[l], subln=subln[l],
                 w_sc=w_sc[l], b_sc=b_sc[l], hy_w1=hy_w1[l], hy_b1=hy_b1[l], hy_fr1=hy_fr1[l],
                 hy_w2=hy_w2[l], hy_b2=hy_b2[l], hy_fr2=hy_fr2[l], hy_w3=hy_w3[l], hy_d=hy_d[l],
                 w_pool=w_pool[l], pool_scale=pool_scale[l], w_out=w_out[l], w_gate=w_gate[l],
                 w_up=w_up[l], w_down=w_down[l])
        lam_init = 0.8 - 0.6 * math.exp(-0.3 * l)
        y_p, k_l, v_l = trunk_layer(y_p, ctx_cond, None, None, None, lam_init, p)
        ks.append(k_l)
        vs.append(v_l)
        y_s, _, _ = trunk_layer(y_s, c, lat_rope, cache_k[:, l], cache_v[:, l], lam_init, p)
    state_k = jnp.stack(ks, axis=1)
    state_v = jnp.stack(vs, axis=1)
    return (y_p, y_s, state_k, state_v)
```

```python
import math
import numpy as np
import ml_dtypes
import concourse.bass as bass
import concourse.mybir as mybir
from concourse.bass_utils import run_bass_kernel_spmd

F32 = mybir.dt.float32
BF16 = mybir.dt.bfloat16
AF = mybir.ActivationFunctionType
ALU = mybir.AluOpType
AX = mybir.AxisListType

D = 2048
KC = 16
DEPTH = 2
LS = 4096
LP = 256
NPS = 4
T = LS + NPS * LP
G = 512
NG = T // G
H = 8
FFN = 5632
HC = FFN // 128
INC = 5120
EPS = 1e-6
NCORES = 8


def _dft_tables(L):
    N = 2 * L
    nch = (L + 1 + 127) // 128
    a = np.arange(nch * 128, dtype=np.int64)
    prod = (a[:, None] * a[None, :]) % N
    ang = 2.0 * np.pi * prod.astype(np.float64) / N
    c = np.cos(ang)
    s = np.sin(ang)
    def tile(m):
        m4 = m.reshape(nch, 128, nch, 128)
        return np.ascontiguousarray(m4.transpose(2, 1, 0, 3)).astype(ml_dtypes.bfloat16)
    f = np.arange(nch * 128)
    g = np.where(f == 0, 1.0, np.where(f < L, 2.0, np.where(f == L, 1.0, 0.0))) / N
    gN = np.ascontiguousarray(g.reshape(nch, 128).T).astype(np.float32)
    return tile(c), tile(s), gN, nch


def _zfeat(L):
    t = np.linspace(0.0, 1.0, L, dtype=np.float32)[:, None]
    wpos = (2.0 * math.pi * np.arange(L, dtype=np.float32)[:, None] / L).astype(np.float32)
    bands = np.linspace(1e-4, 15, 16, dtype=np.float32)[None, :]
    z = np.concatenate([t, np.cos(bands * wpos), np.sin(bands * wpos)], -1).astype(np.float32)
    ntl = np.ascontiguousarray((-t[:, 0]).reshape(-1, 128).T).astype(np.float32)
    return np.ascontiguousarray(z.T), ntl


def _rcount(L):
    t = np.arange(L)
    out = np.zeros((4, L), np.float32)
    for g, w in enumerate((2, 4, 8, 16)):
        lo = np.clip(t - w // 2, 0, L)
        hi = np.clip(t - w // 2 + w, 0, L)
        out[g] = 1.0 / (hi - lo)
    return out


_CONST = None


def _constants():
    global _CONST
    if _CONST is not None:
        return _CONST
    c = {}
    c["ident_f"] = np.eye(128, dtype=np.float32)
    c["ident_b"] = np.eye(128, dtype=np.float32).astype(ml_dtypes.bfloat16)
    c["ones_b"] = np.ones((128, 128), np.float32).astype(ml_dtypes.bfloat16)
    bo = np.zeros((128, 128), np.float32)
    bo[:64, :64] = 1.0
    bo[64:, 64:] = 1.0
    c["bones_b"] = bo.astype(ml_dtypes.bfloat16)
    rp = np.zeros((128, 128), np.float32)
    for m in range(128):
        if m % 64 < 32:
            rp[m + 32, m] = -1.0
        else:
            rp[m - 32, m] = 1.0
    c["rperm_f"] = rp
    tt = np.arange(LS)
    rows = (tt // 64).astype(np.float32)
    cols = (tt % 64).astype(np.float32)
    inv = (10000.0 ** (-np.arange(16, dtype=np.float32) / 16)).astype(np.float32)
    ang = np.concatenate([rows[:, None] * inv, cols[:, None] * inv], -1)
    idx = np.arange(128) % 32
    c["ropec"] = np.ascontiguousarray(np.cos(ang)[:, idx].T).astype(np.float32)
    c["ropes"] = np.ascontiguousarray(np.sin(ang)[:, idx].T).astype(np.float32)
    c["tcs"], c["tss"], c["gns"], _ = _dft_tables(LS)
    c["tcp"], c["tsp"], c["gnp"], _ = _dft_tables(LP)
    c["zfs"], c["ntls"] = _zfeat(LS)
    c["zfp"], c["ntlp"] = _zfeat(LP)
    mn = math.log(1e-2) / 1.5
    mx = math.log(1e-2) / 0.3
    dl = np.abs(np.linspace(mn, mx, 512, dtype=np.float32))
    c["delta"] = np.ascontiguousarray(np.broadcast_to(dl[None, :], (128, 512))).astype(np.float32)
    c["rcs"] = np.ascontiguousarray(np.broadcast_to(_rcount(LS)[None], (128, 4, LS))).astype(np.float32)
    c["rcp"] = np.ascontiguousarray(np.broadcast_to(_rcount(LP)[None], (128, 4, LP))).astype(np.float32)
    _CONST = c
    return c


PP = {}
_off = 0


def _pp(name, n):
    global _off
    PP[name] = (_off, n)
    _off += n


for _l in range(DEPTH):
    _pp(f"bada{_l}", 96)
    _pp(f"n1{_l}", 16)
    _pp(f"n2{_l}", 16)
    _pp(f"qg{_l}", 1)
    _pp(f"kg{_l}", 1)
    _pp(f"wsc{_l}", 36)
    _pp(f"bsc{_l}", 12)
    _pp(f"hyd{_l}", 8)
    _pp(f"psc{_l}", 4)
    _pp(f"hb1{_l}", 1)
    _pp(f"hf1{_l}", 1)
    _pp(f"hb2{_l}", 1)
    _pp(f"hf2{_l}", 1)
    _pp(f"wlam{_l}", 256)
    _pp(f"subln{_l}", 128)
    _pp(f"hydrep{_l}", 1024)
_pp("cvT", 32)
NPP = _off


def _pack_params(inp, core):
    pp = np.zeros((128, NPP), np.float32)

    def put(name, arr):
        o, n = PP[name]
        pp[:, o:o + n] = np.asarray(arr, np.float32).reshape(128, n)

    def cm(v, nch):
        return np.asarray(v, np.float32).reshape(nch, 128).T

    for l in range(DEPTH):
        put(f"bada{l}", cm(inp["b_ada"][l], 96))
        put(f"n1{l}", cm(inp["norm1"][l], 16))
        put(f"n2{l}", cm(inp["norm2"][l], 16))
        put(f"qg{l}", np.tile(inp["q_norm"][l], 2)[:, None])
        put(f"kg{l}", np.tile(inp["k_norm"][l], 2)[:, None])
        put(f"wsc{l}", np.concatenate([cm(inp["w_sc"][l][j], 12) for j in range(3)], 1))
        put(f"bsc{l}", cm(inp["b_sc"][l], 12))
        put(f"hyd{l}", np.concatenate([cm(inp["hy_d"][l][o], 4) for o in range(2)], 1))
        put(f"psc{l}", cm(inp["pool_scale"][l], 4))
        for nm, key in (("hb1", "hy_b1"), ("hf1", "hy_fr1"), ("hb2", "hy_b2"), ("hf2", "hy_fr2")):
            v = np.zeros(128, np.float32)
            v[:64] = inp[key][l]
            put(f"{nm}{l}", v[:, None])
        put(f"wlam{l}", np.broadcast_to(np.asarray(inp["w_lambda"][l]).reshape(1, 256), (128, 256)))
        put(f"subln{l}", np.broadcast_to(np.asarray(inp["subln"][l]).reshape(1, 128), (128, 128)))
        put(f"hydrep{l}", np.broadcast_to(np.asarray(inp["hy_d"][l]).reshape(1, 1024), (128, 1024)))
    cv = np.stack([np.asarray(inp["c_ctx"]), np.asarray(inp["c"][core])], 0)
    cvT = cv.reshape(2, 16, 128).transpose(2, 1, 0).reshape(128, 32)
    put("cvT", cvT)
    return pp


class Buf:
    __slots__ = ("name", "w", "r")

    def __init__(self, name):
        self.name = name
        self.w = None
        self.r = {}


class EngQ:
    def __init__(self, name, eng, key):
        self.name = name
        self.eng = eng
        self.key = key
        self.cnt = 0
        self.seen = {}


class Prog:
    NSLOT = 6

    def __init__(self, nc, es):
        self.nc = nc
        self.sems = {}
        self.q = {}
        for nm, eng in (("pe", nc.tensor), ("dve", nc.vector), ("act", nc.scalar), ("pool", nc.gpsimd), ("sp", nc.sync)):
            key = "e_" + nm
            self.sems[key] = es.enter_context(nc.semaphore("sem_" + key))
            self.q[nm] = EngQ(nm, eng, key)
        self.dq = {}
        for nm in ("sp", "act", "pool"):
            keys = []
            for i in range(self.NSLOT):
                key = f"d_{nm}{i}"
                self.sems[key] = es.enter_context(nc.semaphore("sem_" + key))
                keys.append(key)
            self.dq[nm] = [keys, 0]
        self.semval = {k: 0 for k in self.sems}

    def _waits(self, q, E, reads, writes, extra=()):
        need = {}

        def add(k, v):
            if need.get(k, 0) < v:
                need[k] = v
        for b in reads:
            if b.w is not None:
                add(*b.w)
        for b in writes:
            if b.w is not None:
                add(*b.w)
            for k, v in b.r.items():
                add(k, v)
        for k, v in extra:
            add(k, v)
        for k, v in need.items():
            if k == q.key:
                if E == "pe" or v > q.cnt or q.cnt - v >= 6:
                    continue
            if q.seen.get(k, 0) >= v:
                continue
            q.eng.wait_ge(self.sems[k], v)
            q.seen[k] = v

    def op(self, E, fn, reads=(), writes=(), inc=True):
        q = self.q[E]
        self._waits(q, E, reads, writes)
        ins = fn(q.eng)
        if inc:
            q.cnt += 1
            ins.then_inc(self.sems[q.key], 1)
            self.semval[q.key] = q.cnt
            tick = (q.key, q.cnt)
        else:
            tick = (q.key, q.cnt + 1)
        for b in reads:
            if b.r.get(tick[0], 0) < tick[1]:
                b.r[tick[0]] = tick[1]
        for b in writes:
            b.w = tick
            b.r = {}
        return ins

    def dma(self, Q, out, in_, reads=(), writes=(), **kw):
        q = self.q[Q]
        keys, i = self.dq[Q]
        slot = i % self.NSLOT
        rnd = i // self.NSLOT
        key = keys[slot]
        extra = [(key, 16 * rnd)] if rnd > 0 else []
        self._waits(q, Q, reads, writes, extra)
        q.eng.dma_start(out=out, in_=in_, **kw).then_inc(self.sems[key], 16)
        self.dq[Q][1] = i + 1
        tick = (key, 16 * (rnd + 1))
        self.semval[key] = tick[1]
        for b in reads:
            if b.r.get(key, 0) < tick[1]:
                b.r[key] = tick[1]
        for b in writes:
            b.w = tick
            b.r = {}

    def barrier(self):
        for nm, q in self.q.items():
            for k, v in self.semval.items():
                if v == 0 or k == q.key:
                    continue
                if q.seen.get(k, 0) >= v:
                    continue
                q.eng.wait_ge(self.sems[k], v)
                q.seen[k] = v


class Builder:
    def __init__(self, debug=False):
        self.debug = debug
        self.nc = bass.Bass("TRN2", target_bir_lowering=False)
        self.dr = {}

    def din(self, name, shape, dt=F32):
        self.dr[name] = self.nc.dram_tensor(name, list(shape), dt, kind="ExternalInput").ap()
        return self.dr[name]

    def dout(self, name, shape, dt=F32):
        self.dr[name] = self.nc.dram_tensor(name, list(shape), dt, kind="ExternalOutput").ap()
        return self.dr[name]

    def dscr(self, name, shape, dt):
        kind = "ExternalOutput" if (self.debug and name in DEBUG_OUT) else "Internal"
        self.dr[name] = self.nc.dram_tensor(name, list(shape), dt, kind=kind).ap()
        return self.dr[name]


DEBUG_OUT = ("xT", "qT", "kT", "Vs", "U", "mixT", "X3t", "modd")


def build_program(debug=False, nlayers=DEPTH, stop_after=None):
    import contextlib
    B = Builder(debug)
    nc = B.nc
    cst = _constants()
    xs = B.din("xs", [LS, D])
    xp = B.din("xp", [NPS * LP, D])
    ck = B.din("ck", [DEPTH, LP, 1024])
    cv = B.din("cv", [DEPTH, LP, 1024])
    ppd = B.din("pp", [128, NPP])
    w_ada = B.din("w_ada", [DEPTH, D, 6 * D])
    w_in = B.din("w_in", [DEPTH, D, INC])
    w_out = B.din("w_out", [DEPTH, D, D])
    w_gate = B.din("w_gate", [DEPTH, D, FFN])
    w_up = B.din("w_up", [DEPTH, D, FFN])
    w_down = B.din("w_down", [DEPTH, FFN, D])
    hy_w1 = B.din("hy_w1", [DEPTH, 33, 64])
    hy_w2 = B.din("hy_w2", [DEPTH, 64, 64])
    hy_w3 = B.din("hy_w3", [DEPTH, 64, 2048])
    w_pool = B.din("w_pool", [DEPTH, 4, 128, 128])
    cd = {}
    for k, v in cst.items():
        cd[k] = B.din("c_" + k, v.shape, BF16 if v.dtype == ml_dtypes.bfloat16 else F32)
    ys = B.dout("ys", [LS, D])
    yp = B.dout("yp", [NPS * LP, D])
    sk = B.dout("sk", [NPS, DEPTH, LP, 1024])
    sv = B.dout("sv", [NPS, DEPTH, LP, 1024])
    xT = B.dscr("xT", [D, T], F32)
    wb_in = B.dscr("wb_in", [DEPTH, D, INC], BF16)
    wb_out = B.dscr("wb_out", [DEPTH, D, D], BF16)
    wb_gate = B.dscr("wb_gate", [DEPTH, D, FFN], BF16)
    wb_up = B.dscr("wb_up", [DEPTH, D, FFN], BF16)
    wb_down = B.dscr("wb_down", [DEPTH, FFN, D], BF16)
    qT = B.dscr("qT", [H, 128, T], BF16)
    kT = B.dscr("kT", [H, 128, T], BF16)
    Vs = B.dscr("Vs", [T, H, 130], BF16)
    U = B.dscr("U", [D, T], F32)
    mixT = B.dscr("mixT", [D, T], BF16)
    X3t = B.dscr("X3t", [3, T, 512], BF16)
    Kfs = B.dscr("Kfs", [2, 33, 128, 2, 512], BF16)
    Kfp = B.dscr("Kfp", [2, 3, 128, 2, 512], BF16)
    modd = B.dscr("modd", [DEPTH, 128, 192], F32)

    es = contextlib.ExitStack()
    with es:
        P = Prog(nc, es)
        ps = es.enter_context(nc.psum_tensor("ps", [128, 8, 512], F32))
        pbuf = [Buf(f"ps{i}") for i in range(8)]
        pstate = [0]

        held = set()

        def nb():
            while True:
                i = pstate[0] % 8
                pstate[0] += 1
                if i not in held:
                    return ps[:, i, :], pbuf[i]

        def hold(n):
            out = []
            for _ in range(n):
                ap, b = nb()
                held.add(pbuf.index(b))
                out.append((ap, b))
            return out

        def release(banks):
            for ap, b in banks:
                held.discard(pbuf.index(b))

        uniq = [0]

        def sb(name, shape, dt):
            uniq[0] += 1
            t = es2.enter_context(nc.sbuf_tensor(f"{name}_{uniq[0]}", list(shape), dt))
            return t, Buf(name)

        blk = es.enter_context(nc.Block())

        es2 = es
        ppt, ppb = sb("ppt", [128, NPP], F32)
        idf, idfb = sb("idf", [128, 128], F32)
        idb, idbb = sb("idb", [128, 128], BF16)
        oneb, onebb = sb("oneb", [128, 128], BF16)
        boneb, bonebb = sb("boneb", [128, 128], BF16)
        rpf, rpfb = sb("rpf", [128, 128], F32)
        modt, modb = sb("modt", [128, DEPTH, 96, 2], F32)
        gmt, gmb = sb("gmt", [128, DEPTH, 2, 16, 2], F32)
        lamt, lamb = sb("lamt", [128, DEPTH, 2], F32)
        sct, scb = sb("sct", [128, 16, 2], F32)
        P.dma("sp", ppt[:], ppd[:, :], writes=[ppb])
        P.dma("sp", idf[:], cd["ident_f"][:, :], writes=[idfb])
        P.dma("sp", idb[:], cd["ident_b"][:, :], writes=[idbb])
        P.dma("sp", oneb[:], cd["ones_b"][:, :], writes=[onebb])
        P.dma("sp", boneb[:], cd["bones_b"][:, :], writes=[bonebb])
        P.dma("sp", rpf[:], cd["rperm_f"][:, :], writes=[rpfb])

        def ppa(name, i=0, n=1):
            o, _ = PP[name]
            return ppt[:, o + i:o + i + n]

        def conv_items(l, which):
            specs = {"in": (w_in, wb_in, D, INC), "out": (w_out, wb_out, D, D), "gate": (w_gate, wb_gate, D, FFN),
                     "up": (w_up, wb_up, D, FFN), "down": (w_down, wb_down, FFN, D)}
            items = []
            for nm in which:
                src, dst, rows, cols = specs[nm]
                nrc = rows // 128
                for c0 in range(0, cols, 512):
                    for r0 in range(0, nrc, 8):
                        nr = min(8, nrc - r0)
                        sv_ = src[l].rearrange("(kc p) n -> p kc n", p=128)[:, r0:r0 + nr, c0:c0 + 512]
                        dv_ = dst[l].rearrange("(kc p) n -> p kc n", p=128)[:, r0:r0 + nr, c0:c0 + 512]
                        items.append((sv_, dv_, nr))
            return items

        def phase_W(l, which):
            nonlocal es2
            with contextlib.ExitStack() as es2:
                stg = [sb(f"wstg{i}", [128, 8, 512], F32) for i in range(3)]
                cvt = [sb(f"wcvt{i}", [128, 8, 512], BF16) for i in range(3)]
                i = 0
                specs = {"in": (w_in, wb_in, D, INC), "out": (w_out, wb_out, D, D), "gate": (w_gate, wb_gate, D, FFN),
                         "up": (w_up, wb_up, D, FFN), "down": (w_down, wb_down, FFN, D)}
                for nm in which:
                    src, dst, rows, cols = specs[nm]
                    nrc = rows // 128
                    for c0 in range(0, cols, 512):
                        for r0 in range(0, nrc, 8):
                            nr = min(8, nrc - r0)
                            st_, stb_ = stg[i % 3]
                            cv_, cvb_ = cvt[i % 3]
                            sv_ = src[l].rearrange("(kc p) n -> p kc n", p=128)[:, r0:r0 + nr, c0:c0 + 512]
                            dv_ = dst[l].rearrange("(kc p) n -> p kc n", p=128)[:, r0:r0 + nr, c0:c0 + 512]
                            P.dma("sp", st_[:, 0:nr, :], sv_, writes=[stb_])
                            eng = ("dve", "pool", "act")[i % 3]
                            if eng == "act":
                                P.op("act", lambda e: e.activation(out=cv_[:, 0:nr, :], in_=st_[:, 0:nr, :], func=AF.Copy), reads=[stb_], writes=[cvb_])
                            else:
                                P.op(eng, lambda e: e.tensor_copy(out=cv_[:, 0:nr, :], in_=st_[:, 0:nr, :]), reads=[stb_], writes=[cvb_])
                            P.dma("act", dv_, cv_[:, 0:nr, :], reads=[cvb_])
                            i += 1
                P.barrier()

        def wait_conv(name, l, engs=("sp",)):
            pass

        def phase_mod(l):
            nonlocal es2
            with contextlib.ExitStack() as es2:
                wa = [sb(f"wa{i}", [128, 16, 512], F32) for i in range(2)]
                cvs, cvsb = sb("cvs", [128, 32], F32)
                P.op("act", lambda e: e.activation(out=cvs[:], in_=ppa("cvT", 0, 32), func=AF.Silu), reads=[ppb], writes=[cvsb])
                P.op("dve", lambda e: e.tensor_copy(out=sct[:].rearrange("p a b -> p (a b)"), in_=cvs[:]), reads=[cvsb], writes=[scb])
                for l in [l]:
                    bank, bb = nb()
                    for pc in range(24):
                        wt, wtb = wa[pc % 2]
                        src = w_ada[l].rearrange("(kc p) n -> p kc n", p=128)[:, :, pc * 512:(pc + 1) * 512]
                        P.dma("sp", wt[:], src, writes=[wtb])
                        for j in range(4):
                            m = pc * 4 + j
                            for kc in range(KC):
                                P.op("pe", lambda e: e.matmul(bank[:, 2 * m:2 * m + 2], lhsT=wt[:, kc, j * 128:(j + 1) * 128],
                                                              rhs=sct[:, kc, :], start=(kc == 0), stop=(kc == KC - 1)),
                                     reads=[wtb, scb], writes=[bb], inc=(kc == KC - 1))
                    o, _ = PP[f"bada{l}"]
                    P.op("dve", lambda e: e.tensor_tensor(out=modt[:, l], in0=bank[:, 0:192].rearrange("p (m c) -> p m c", c=2),
                                                          in1=ppt[:, o:o + 96].unsqueeze(2).to_broadcast([128, 96, 2]), op=ALU.add),
                         reads=[bb, ppb], writes=[modb])
                    for wh, nm, sc in ((0, "n1", 1), (1, "n2", 4)):
                        o2, _ = PP[f"{nm}{l}"]
                        P.op("dve", lambda e: e.scalar_tensor_tensor(out=gmt[:, l, wh], in0=modt[:, l, sc * 16:(sc + 1) * 16, :], scalar=1.0,
                                                                     in1=ppt[:, o2:o2 + 16].unsqueeze(2).to_broadcast([128, 16, 2]),
                                                                     op0=ALU.add, op1=ALU.mult),
                             reads=[modb, ppb], writes=[gmb])
                    o3, _ = PP[f"wlam{l}"]
                    tl, tlb = sb(f"tl{l}", [128, 128], F32)
                    sl, slb = sb(f"sl{l}", [128, 2], F32)
                    P.op("dve", lambda e: e.tensor_tensor(out=tl[:].rearrange("p (a b) -> p a b", a=2),
                                                          in0=ppt[:, o3:o3 + 256].rearrange("p (a r b) -> p a r b", a=2, r=2)[:, :, 0, :],
                                                          in1=ppt[:, o3:o3 + 256].rearrange("p (a r b) -> p a r b", a=2, r=2)[:, :, 1, :], op=ALU.mult),
                         reads=[ppb], writes=[tlb])
                    P.op("dve", lambda e: e.tensor_reduce(out=sl[:], in_=tl[:].rearrange("p (a b) -> p a b", a=2), axis=AX.X, op=ALU.add),
                         reads=[tlb], writes=[slb])
                    P.op("act", lambda e: e.activation(out=sl[:], in_=sl[:], func=AF.Exp), reads=[slb], writes=[slb])
                    lam_init = 0.8 - 0.6 * math.exp(-0.3 * l)
                    P.op("dve", lambda e: e.scalar_tensor_tensor(out=lamt[:, l, 0:1], in0=sl[:, 0:1], scalar=lam_init, in1=sl[:, 1:2],
                                                                 op0=ALU.add, op1=ALU.subtract), reads=[slb], writes=[lamb])
                    P.op("dve", lambda e: e.tensor_scalar(out=lamt[:, l, 1:2], in0=lamt[:, l, 0:1], scalar1=-1.0, scalar2=None, op0=ALU.mult),
                         reads=[lamb], writes=[lamb])
                    if debug:
                        P.dma("act", modd[l], modt[:, l].rearrange("p m c -> p (m c)"), reads=[modb])
                P.barrier()

        def phase_xT():
            nonlocal es2
            with contextlib.ExitStack() as es2:
                xin = [sb(f"xin{i}", [128, D], F32) for i in range(2)]
                xst = [sb(f"xst{i}", [128, KC, G], F32) for i in range(2)]
                it = 0
                for g in range(NG):
                    st, stb = xst[g % 2]
                    for tt in range(4):
                        tok = g * G + tt * 128
                        xi, xib = xin[it % 2]
                        it += 1
                        src = xs[tok:tok + 128, :] if tok < LS else xp[tok - LS:tok - LS + 128, :]
                        P.dma("sp", xi[:], src, writes=[xib])
                        for k4 in range(4):
                            bank, bb = nb()
                            for j in range(4):
                                kc = k4 * 4 + j
                                P.op("pe", lambda e: e.transpose(bank[:, j * 128:(j + 1) * 128], xi[:, kc * 128:(kc + 1) * 128], idf[:]),
                                     reads=[xib, idfb], writes=[bb], inc=(j == 3))
                            eng = "act" if k4 % 2 == 0 else "dve"
                            if eng == "act":
                                P.op("act", lambda e: e.activation(out=st[:, k4 * 4:(k4 + 1) * 4, tt * 128:(tt + 1) * 128],
                                                                   in_=bank.rearrange("p (a b) -> p a b", a=4), func=AF.Copy),
                                     reads=[bb], writes=[stb])
                            else:
                                P.op("dve", lambda e: e.tensor_copy(out=st[:, k4 * 4:(k4 + 1) * 4, tt * 128:(tt + 1) * 128],
                                                                    in_=bank.rearrange("p (a b) -> p a b", a=4)),
                                     reads=[bb], writes=[stb])
                    P.dma("act", xT.rearrange("(kc p) t -> p kc t", p=128)[:, :, g * G:(g + 1) * G], st[:], reads=[stb])
                P.barrier()

        def norm_mod(xg, xgb, xn, xnb, hT, hTb, sqb_t, sqbb, rst, rstb, l, wh, cond):
            xgl = xgb if isinstance(xgb, list) else [xgb]
            bank, bb = nb()
            for q4 in range(4):
                rd = [xgl[q4]] if len(xgl) == 4 else xgl
                P.op("act", lambda e: e.activation(out=sqb_t[:, q4 * 4:(q4 + 1) * 4, :], in_=xg[:, q4 * 4:(q4 + 1) * 4, :], func=AF.Square), reads=rd, writes=[sqbb])
                for kc in range(q4 * 4, q4 * 4 + 4):
                    P.op("pe", lambda e: e.matmul(bank, lhsT=oneb[:], rhs=sqb_t[:, kc, :], start=(kc == 0), stop=(kc == KC - 1)),
                         reads=[sqbb, onebb], writes=[bb], inc=(kc % 4 == 3))
            P.op("act", lambda e: e.activation(out=rst[:], in_=bank, func=AF.Ln, bias=EPS, scale=1.0 / D), reads=[bb], writes=[rstb])
            P.op("act", lambda e: e.activation(out=rst[:], in_=rst[:], func=AF.Exp, scale=-0.5), reads=[rstb], writes=[rstb])
            shi = 0 if wh == 0 else 3
            xnq = [Buf(f"xnq{i}") for i in range(4)]
            for q4 in range(4):
                P.op("dve", lambda e: e.tensor_tensor(out=xn[:, q4 * 4:(q4 + 1) * 4, :], in0=xg[:, q4 * 4:(q4 + 1) * 4, :],
                                                      in1=rst[:].unsqueeze(1).to_broadcast([128, 4, G]), op=ALU.mult),
                     reads=xgl + [rstb, sqbb], writes=[xnq[q4], xnb])
                for kc in range(q4 * 4, q4 * 4 + 4):
                    P.op("act", lambda e: e.activation(out=hT[:, kc, :], in_=xn[:, kc, :], func=AF.Identity,
                                                       scale=gmt[:, l, wh, kc, cond:cond + 1], bias=modt[:, l, shi * 16 + kc, cond:cond + 1]),
                         reads=[xnq[q4], xnb, gmb, modb], writes=[hTb])

        def phase_A(l):
            nonlocal es2
            wait_conv("in", l)
            with contextlib.ExitStack() as es2:
                xgs = [sb(f"xg{i}", [128, KC, G], F32) for i in range(1)]
                hT, hTb = sb("hT", [128, KC, G], BF16)
                rst, rstb = sb("rst", [128, G], F32)
                wsl = [sb(f"wsl{i}", [128, KC, 512], BF16) for i in range(3)]
                sqh = [sb(f"sqh{i}", [128, G], BF16) for i in range(2)]
                rsh = [sb(f"rsh{i}", [128, G], F32) for i in range(2)]
                qgs = [sb(f"qgs{i}", [128, G], F32) for i in range(2)]
                t1s = [sb(f"t1s{i}", [128, G], F32) for i in range(2)]
                t2s = [sb(f"t2s{i}", [128, G], F32) for i in range(2)]
                qob = [sb(f"qob{i}", [128, G], BF16) for i in range(3)]
                kn32 = [sb(f"kn32{i}", [128, G], F32) for i in range(2)]
                rc, rcb = sb("rc", [128, G], F32)
                rs, rsb = sb("rs", [128, G], F32)
                vst, vstb = sb("vst", [128, 4, H, 130], BF16)
                svst, svstb = sb("svst", [128, 4, 1024], F32)
                skst, skstb = sb("skst", [128, 4, 1024], F32)
                ust = [sb(f"ust{i}", [128, G], F32) for i in range(3)]
                P.op("pool", lambda e: e.memset(vst[:, :, :, 128:130], 1.0), writes=[vstb])
                wi = 0
                ui = 0
                hi = 0
                dq1, dq2, dq2n = [], [], []
                prefetched = [False]

                def dtick():
                    for f_ in dq2:
                        f_()
                    dq2.clear()
                    for f_ in dq1:
                        f_()
                    dq1.clear()
                    dq2.extend(dq2n)
                    dq2n.clear()

                import os as _os
                _gl = [int(x) for x in _os.environ.get("KA_G", ",".join(str(i) for i in range(NG))).split(",")]
                _sl = [int(x) for x in _os.environ.get("KA_S", "0,1,2,3,4,5,6,7,8,9").split(",")]
                for g in _gl:
                    tok0 = g * G
                    samp = g < 8
                    cond = 1 if samp else 0
                    xg, xgb = xgs[0]
                    if not prefetched[0]:
                        P.dma("sp", xg[:], xT.rearrange("(kc p) t -> p kc t", p=128)[:, :, tok0:tok0 + G], writes=[xgb])
                    prefetched[0] = False
                    if samp:
                        P.dma("sp", rc[:], cd["ropec"][:, tok0:tok0 + G], writes=[rcb])
                        P.dma("sp", rs[:], cd["ropes"][:, tok0:tok0 + G], writes=[rsb])
                    norm_mod(xg, xgb, xg, xgb, hT, hTb, hT, hTb, rst, rstb, l, 0, cond)
                    for s in _sl:
                        if s == 3 and len(_gl) == NG and g + 1 < NG:
                            P.dma("sp", xg[:], xT.rearrange("(kc p) t -> p kc t", p=128)[:, :, tok0 + G:tok0 + 2 * G], writes=[xgb])
                            prefetched[0] = True
                        wt, wtb = wsl[wi % 3]
                        wi += 1
                        P.dma("sp", wt[:], wb_in[l].rearrange("(kc p) n -> p kc n", p=128)[:, :, s * 512:(s + 1) * 512], writes=[wtb])
                        if s in (4, 5):
                            for tt in range(4):
                                bank, bb = nb()
                                for kc in range(KC):
                                    P.op("pe", lambda e: e.matmul(bank, lhsT=hT[:, kc, tt * 128:(tt + 1) * 128], rhs=wt[:, kc, :],
                                                                  start=(kc == 0), stop=(kc == KC - 1)),
                                         reads=[hTb, wtb], writes=[bb], inc=(kc == KC - 1))
                                h0 = (s - 4) * 4
                                P.op("act", lambda e: e.activation(out=vst[:, tt, h0:h0 + 4, 0:128], in_=bank.rearrange("p (a b) -> p a b", a=4),
                                                                   func=AF.Copy), reads=[bb], writes=[vstb])
                                dtick()
                                if not samp and not _os.environ.get("KA_NOSVCOPY"):
                                    P.op("act", lambda e: e.activation(out=svst[:, tt, h0 * 128:(h0 + 4) * 128], in_=bank, func=AF.Copy), reads=[bb], writes=[svstb])
                            if s == 5:
                                for tt in range(4):
                                    tk = tok0 + tt * 128
                                    P.dma("act", Vs[tk:tk + 128].rearrange("t h e -> t (h e)"), vst[:, tt].rearrange("p h e -> p (h e)"), reads=[vstb])
                                    if not samp and not _os.environ.get("KA_NOSVDMA"):
                                        sq_ = (tk - LS) // LP
                                        t_ = (tk - LS) % LP
                                        P.dma("act", sv[sq_, l, t_:t_ + 128, :], svst[:, tt, :], reads=[svstb])
                            continue
                        for j in range(4):
                            cc = s * 4 + j
                            bank, bb = nb()
                            for kc in range(KC):
                                P.op("pe", lambda e: e.matmul(bank, lhsT=wt[:, kc, j * 128:(j + 1) * 128], rhs=hT[:, kc, :],
                                                              start=(kc == 0), stop=(kc == KC - 1)),
                                     reads=[hTb, wtb], writes=[bb], inc=(kc == KC - 1))
                            if s >= 6:
                                ut, utb = ust[ui % 3]
                                ui += 1
                                P.op("act", lambda e: e.activation(out=ut[:], in_=bank, func=AF.Copy), reads=[bb], writes=[utb])
                                ch = (cc - 24) * 128
                                P.dma("act", U[ch:ch + 128, tok0:tok0 + G], ut[:], reads=[utb])
                                dtick()
                                continue
                            isk = s >= 2
                            head = cc % 8
                            sq_t, sq_b = sqh[hi % 2]
                            rh, rhb = rsh[hi % 2]
                            qg_t, qg_b = qgs[hi % 2]
                            t1, t1b = t1s[hi % 2]
                            t2, t2b = t2s[hi % 2]
                            qo, qo_b = qob[hi % 3]
                            k32, k32b = kn32[hi % 2]
                            hi += 1
                            gname = f"kg{l}" if isk else f"qg{l}"
                            P.op("act", lambda e: e.activation(out=sq_t[:], in_=bank, func=AF.Square), reads=[bb], writes=[sq_b])
                            P.op("act", lambda e: e.activation(out=qg_t[:], in_=bank, func=AF.Identity, scale=ppa(gname)), reads=[bb, ppb], writes=[qg_b])

                            def post1(sq_t=sq_t, sq_b=sq_b, rh=rh, rhb=rhb, qg_t=qg_t, qg_b=qg_b, t1=t1, t1b=t1b, t2=t2, t2b=t2b,
                                      qo=qo, qo_b=qo_b, k32=k32, k32b=k32b, head=head, isk=isk, samp=samp, tok0=tok0):
                                bk2, bb2 = nb()
                                P.op("pe", lambda e: e.matmul(bk2, lhsT=boneb[:], rhs=sq_t[:], start=True, stop=True), reads=[sq_b, bonebb], writes=[bb2])
                                P.op("act", lambda e: e.activation(out=rh[:], in_=bk2, func=AF.Ln, bias=EPS, scale=1.0 / 64), reads=[bb2], writes=[rhb])
                                P.op("act", lambda e: e.activation(out=rh[:], in_=rh[:], func=AF.Exp, scale=-0.5), reads=[rhb], writes=[rhb])
                                if samp:
                                    bk3, bb3 = nb()
                                    P.op("pe", lambda e: e.matmul(bk3, lhsT=rpf[:], rhs=qg_t[:], start=True, stop=True), reads=[qg_b, rpfb], writes=[bb3])
                                    P.op("dve", lambda e: e.tensor_tensor(out=t1[:], in0=qg_t[:], in1=rc[:], op=ALU.mult), reads=[qg_b, rcb], writes=[t1b])
                                    P.op("dve", lambda e: e.tensor_tensor(out=t2[:], in0=bk3, in1=rs[:], op=ALU.mult), reads=[bb3, rsb], writes=[t2b])
                                    P.op("dve", lambda e: e.tensor_tensor(out=t1[:], in0=t1[:], in1=t2[:], op=ALU.add), reads=[t1b, t2b], writes=[t1b])
                                    P.op("dve", lambda e: e.tensor_tensor(out=qo[:], in0=t1[:], in1=rh[:], op=ALU.mult), reads=[t1b, rhb], writes=[qo_b])
                                else:
                                    P.op("dve", lambda e: e.tensor_tensor(out=qo[:], in0=qg_t[:], in1=rh[:], op=ALU.mult), reads=[qg_b, rhb], writes=[qo_b])
                                    if isk:
                                        P.op("dve", lambda e: e.tensor_tensor(out=k32[:], in0=qg_t[:], in1=rh[:], op=ALU.mult), reads=[qg_b, rhb], writes=[k32b])

                                        def post2(k32=k32, k32b=k32b, head=head):
                                            bk4, bb4 = nb()
                                            for tt in range(4):
                                                P.op("pe", lambda e: e.transpose(bk4[:, tt * 128:(tt + 1) * 128], k32[:, tt * 128:(tt + 1) * 128], idf[:]),
                                                     reads=[k32b, idfb], writes=[bb4], inc=(tt == 3))
                                            P.op("act", lambda e: e.activation(out=skst[:, :, head * 128:(head + 1) * 128],
                                                                               in_=bk4.rearrange("p (a b) -> p a b", a=4), func=AF.Copy), reads=[bb4], writes=[skstb])
                                        dq2n.append(post2)
                                dst = (kT if isk else qT)[head, :, tok0:tok0 + G]
                                P.dma("act", dst, qo[:], reads=[qo_b])
                            dtick()
                            dq1.append(post1)
                        if s == 3 and not samp:
                            dtick()
                            dtick()
                            for tt in range(4):
                                tk = tok0 + tt * 128
                                sq_ = (tk - LS) // LP
                                t_ = (tk - LS) % LP
                                P.dma("act", sk[sq_, l, t_:t_ + 128, :], skst[:, tt, :], reads=[skstb])
                    dtick()
                    dtick()
                P.barrier()

        def phase_B(l):
            nonlocal es2
            lam_init = 0.8 - 0.6 * math.exp(-0.3 * l)
            with contextlib.ExitStack() as es2:
                ckl, cklb = sb("ckl", [128, 2, 1024], F32)
                ckT, ckTb = sb("ckT", [128, H, LP], BF16)
                cvb, cvbb = sb("cvb", [128, 2, H, 130], BF16)
                sln, slnb = sb("sln", [128, 128], F32)
                kts = [sb(f"kts{i}", [128, LS + LP], BF16) for i in range(2)]
                qts = [sb(f"qts{i}", [128, 2, LS], BF16) for i in range(2)]
                vts = [sb(f"vts{i}", [128, 34, 130], BF16) for i in range(2)]
                pts = [sb(f"pts{i}", [128, 2, 512], BF16) for i in range(4)]
                ats, atsb = sb("ats", [128, 512], BF16)
                o1s = [sb(f"o1s{i}", [128, 128], F32) for i in range(4)]
                ars = [sb(f"ars{i}", [128, 128], F32) for i in range(4)]
                abs_ = [sb(f"abs{i}", [128, 128], BF16) for i in range(4)]
                rr = [sb(f"rr{i}", [128, 4], F32) for i in range(4)]
                P.dma("sp", ckl[:], ck[l].rearrange("(a p) n -> p a n", p=128), writes=[cklb])
                for tc in range(2):
                    for h in range(H):
                        bank, bb = nb()
                        P.op("pe", lambda e: e.transpose(bank[:, 0:128], ckl[:, tc, h * 128:(h + 1) * 128], idf[:]), reads=[cklb, idfb], writes=[bb])
                        P.op("act", lambda e: e.activation(out=ckT[:, h, tc * 128:(tc + 1) * 128], in_=bank[:, 0:128], func=AF.Copy), reads=[bb], writes=[ckTb])
                ckl2, ckl2b = sb("ckl2", [128, 2, 1024], F32)
                P.dma("sp", ckl2[:], cv[l].rearrange("(a p) n -> p a n", p=128), writes=[ckl2b])
                P.op("pool", lambda e: e.memset(cvb[:, :, :, 128:130], 1.0), writes=[cvbb])
                P.op("dve", lambda e: e.tensor_copy(out=cvb[:, :, :, 0:128], in_=ckl2[:].rearrange("p a (h e) -> p a h e", h=H)), reads=[ckl2b], writes=[cvbb])
                o, _ = PP[f"subln{l}"]
                P.op("dve", lambda e: e.tensor_scalar(out=sln[:], in0=ppt[:, o:o + 128], scalar1=(1.0 - lam_init), scalar2=None, op0=ALU.mult),
                     reads=[ppb], writes=[slnb])
                seqs = [(0, LS, True)] + [(LS + s * LP, LP, False) for s in range(NPS)]
                it = 0
                pi = 0
                ei = 0
                qbi = 0
                sci = 0
                bstg = [sb(f"bstg{i}", [128, 8, 512], F32) for i in range(2)]
                bcvt = [sb(f"bcvt{i}", [128, 8, 512], BF16) for i in range(2)]
                bgq = conv_items(l, ["out", "gate", "up", "down"]) + (conv_items(l + 1, ["in"]) if l + 1 < nlayers else [])
                bgstate = [0, None]

                def bg_step():
                    if bgstate[1] is not None:
                        dv_, cv_, cvb_, nr = bgstate[1]
                        P.dma("sp", dv_, cv_[:, 0:nr, :], reads=[cvb_])
                        bgstate[1] = None
                    if bgq:
                        sv_, dv_, nr = bgq.pop(0)
                        i_ = bgstate[0]
                        bgstate[0] += 1
                        st_, stb_ = bstg[i_ % 2]
                        cv_, cvb_ = bcvt[i_ % 2]
                        P.dma("sp", st_[:, 0:nr, :], sv_, writes=[stb_])
                        P.op("pool", lambda e: e.tensor_copy(out=cv_[:, 0:nr, :], in_=st_[:, 0:nr, :]), reads=[stb_], writes=[cvb_])
                        bgstate[1] = (dv_, cv_, cvb_, nr)
                held.update(range(7))
                for (qz_, qzb_) in qts:
                    P.op("pool", lambda e: e.memset(qz_[:], 0.0), writes=[qzb_])
                QB = 256
                nqt = 2
                pend = []
                atsl = [sb(f"atsd{i}", [128, 256], BF16) for i in range(2)]
                epc = [0]

                def make_epi(regs, h, tok0, q0):
                    at_, atb_ = atsl[epc[0] % 2]
                    epc[0] += 1
                    bufsets = []
                    for qi in range(nqt):
                        k_ = (epc[0] * 2 + qi) % 4
                        bufsets.append(k_)

                    def st1():
                        for qi in range(nqt):
                            (r1, rb1, _), (r2, rb2, _) = regs[qi * 2], regs[qi * 2 + 1]
                            k_ = bufsets[qi]
                            o1, o1b = o1s[k_]
                            ar, arb = ars[k_]
                            rt, rtb = rr[k_]
                            P.op("dve", lambda e: e.reciprocal(out=rt[:, 0:1], in_=r1[:, 128:129]), reads=[rb1], writes=[rtb])
                            P.op("dve", lambda e: e.reciprocal(out=rt[:, 1:2], in_=r2[:, 128:129]), reads=[rb2], writes=[rtb])
                            P.op("dve", lambda e: e.tensor_tensor(out=rt[:, 1:2], in0=rt[:, 1:2], in1=lamt[:, l, 1:2], op=ALU.mult), reads=[rtb, lamb], writes=[rtb])
                            P.op("dve", lambda e: e.tensor_scalar(out=o1[:], in0=r1[:, 0:128], scalar1=rt[:, 0:1], scalar2=None, op0=ALU.mult), reads=[rb1, rtb], writes=[o1b])
                            P.op("dve", lambda e: e.scalar_tensor_tensor(out=ar[:], in0=r2[:, 0:128], scalar=rt[:, 1:2], in1=o1[:], op0=ALU.mult, op1=ALU.add),
                                 reads=[rb2, rtb, o1b], writes=[arb])

                    def st2():
                        for qi in range(nqt):
                            k_ = bufsets[qi]
                            o1, o1b = o1s[k_]
                            ar, arb = ars[k_]
                            rt, rtb = rr[k_]
                            P.op("act", lambda e: e.activation(out=o1[:], in_=ar[:], func=AF.Square, accum_out=rt[:, 2:3]), reads=[arb], writes=[o1b, rtb])
                            P.op("act", lambda e: e.activation(out=rt[:, 3:4], in_=rt[:, 2:3], func=AF.Ln, bias=EPS, scale=1.0 / 128), reads=[rtb], writes=[rtb])
                            P.op("act", lambda e: e.activation(out=rt[:, 3:4], in_=rt[:, 3:4], func=AF.Exp, scale=-0.5), reads=[rtb], writes=[rtb])

                    def st3():
                        for qi in range(nqt):
                            k_ = bufsets[qi]
                            ar, arb = ars[k_]
                            ab, abb = abs_[k_]
                            rt, rtb = rr[k_]
                            P.op("dve", lambda e: e.scalar_tensor_tensor(out=ab[:], in0=ar[:], scalar=rt[:, 3:4], in1=sln[:], op0=ALU.mult, op1=ALU.mult),
                                 reads=[arb, rtb, slnb], writes=[abb])
                            bk5, bb5 = nb()
                            bk5b = bk5.bitcast(BF16)
                            P.op("pe", lambda e: e.transpose(bk5b[:, 0:128], ab[:], idb[:]), reads=[abb, idbb], writes=[bb5])
                            P.op("dve", lambda e: e.tensor_copy(out=at_[:, qi * 128:(qi + 1) * 128], in_=bk5b[:, 0:128]), reads=[bb5], writes=[atb_])
                        P.dma("sp", mixT[h * 128:(h + 1) * 128, tok0 + q0:tok0 + q0 + QB], at_[:, 0:QB], reads=[atb_])
                    return [st1, st2, st3]

                for (tok0, L, cache) in seqs:
                    nkc = L // 128 + (2 if cache else 0)
                    for h in range(H):
                        kt, ktb = kts[it % 2]
                        qt_, qtb = qts[it % 2]
                        vt, vtb = vts[it % 2]
                        it += 1
                        P.dma("sp", kt[:, 0:L], kT[h, :, tok0:tok0 + L], writes=[ktb])
                        P.dma("sp", qt_[0:64, 0, 0:L], qT[h, 0:64, tok0:tok0 + L], writes=[qtb])
                        P.dma("sp", qt_[64:128, 1, 0:L], qT[h, 64:128, tok0:tok0 + L], writes=[qtb])
                        P.dma("sp", vt[:, 0:L // 128, :], Vs[tok0:tok0 + L, h, :].rearrange("(a p) e -> p a e", p=128), writes=[vtb])
                        if cache:
                            P.op("pool", lambda e: e.tensor_copy(out=kt[:, L:L + LP], in_=ckT[:, h, :]), reads=[ckTb], writes=[ktb])
                            P.op("pool", lambda e: e.tensor_copy(out=vt[:, L // 128:L // 128 + 2, :], in_=cvb[:, :, h, :]), reads=[cvbb], writes=[vtb])
                        for qb in range(L // QB):
                            q0 = qb * QB
                            aset = qbi % 2
                            qbi += 1
                            bg_step()
                            while len(pend) > 1:
                                for stg_ in pend.pop(0):
                                    stg_()
                            bA, bB = 2 * aset, 2 * aset + 1
                            regs = [(ps[:, bA, 0:130], pbuf[bA], True), (ps[:, bA, 130:260], pbuf[bA], False),
                                    (ps[:, bA, 260:390], pbuf[bA], False), (ps[:, bB, 0:130], pbuf[bB], True)]
                            sbanks = {}

                            def emit_S(kc):
                                nonlocal sci
                                bi_ = 4 + (sci % 3)
                                sci += 1
                                sbanks[kc] = (ps[:, bi_, :], pbuf[bi_])
                                P.op("pe", lambda e: e.matmul(ps[:, bi_, :].rearrange("p (a b) -> p a b", a=2), lhsT=kt[:, kc * 128:(kc + 1) * 128],
                                                              rhs=qt_[:, :, q0:q0 + QB], start=True, stop=True),
                                     reads=[ktb, qtb], writes=[pbuf[bi_]], inc=True)
                            emit_S(0)
                            if nkc > 1:
                                emit_S(1)
                            for kc in range(nkc):
                                if pend and kc in (2, 6, 10):
                                    pend[0].pop(0)()
                                    if not pend[0]:
                                        pend.pop(0)
                                if kc + 2 < nkc:
                                    emit_S(kc + 2)
                                bk, bkb = sbanks.pop(kc)
                                pt, ptb = pts[pi % 4]
                                pi += 1
                                P.op("act", lambda e: e.activation(out=pt[:].rearrange("p a b -> p (a b)")[:, 0:512], in_=bk, func=AF.Exp, scale=0.125), reads=[bkb], writes=[ptb])
                                ptf = pt[:].rearrange("p a b -> p (a b)")
                                for qi in range(nqt):
                                    for j in range(2):
                                        ra, rb_, first = regs[qi * 2 + j]
                                        c0 = j * 256 + qi * 128
                                        P.op("pe", lambda e: e.matmul(ra, lhsT=ptf[:, c0:c0 + 128], rhs=vt[:, kc, :],
                                                                      start=(kc == 0 and first), stop=(kc == nkc - 1), skip_group_check=True),
                                             reads=[ptb, vtb], writes=[rb_], inc=(kc == nkc - 1))
                            pend.append(make_epi(regs, h, tok0, q0))
                while pend:
                    for stg_ in pend.pop(0):
                        stg_()
                while bgq or bgstate[1] is not None:
                    bg_step()
                held.clear()
                P.barrier()

        def phase_C1(l):
            nonlocal es2
            with contextlib.ExitStack() as es2:
                ubuf = [sb(f"ub{i}", [128, 8 + T + 8], F32) for i in range(2)]
                cvo, cvob = sb("cvo", [128, T], F32)
                tst = [sb(f"tst{i}", [128, 4, 128], BF16) for i in range(3)]
                seqs = [(0, LS)] + [(LS + s * LP, LP) for s in range(NPS)]
                ti = 0
                for ch in range(12):
                    ub, ubb = ubuf[ch % 2]
                    P.op("pool", lambda e: e.memset(ub[:], 0.0), writes=[ubb])
                    offs = []
                    pos = 1
                    for (tok0, L) in seqs:
                        offs.append(pos)
                        P.dma("sp", ub[:, pos:pos + L], U[ch * 128:(ch + 1) * 128, tok0:tok0 + L], writes=[ubb])
                        pos += L + 1
                    o, _ = PP[f"wsc{l}"]
                    ob, _ = PP[f"bsc{l}"]
                    for si, (tok0, L) in enumerate(seqs):
                        p0 = offs[si]
                        eng = "dve"
                        P.op("act", lambda e: e.activation(out=cvo[:, tok0:tok0 + L], in_=ub[:, p0:p0 + L], func=AF.Identity,
                                                           scale=ppt[:, o + 12 + ch:o + 12 + ch + 1], bias=ppt[:, ob + ch:ob + ch + 1]),
                             reads=[ubb, ppb], writes=[cvob])
                        P.op("dve", lambda e: e.scalar_tensor_tensor(out=cvo[:, tok0:tok0 + L], in0=ub[:, p0 - 1:p0 - 1 + L], scalar=ppt[:, o + ch:o + ch + 1],
                                                                     in1=cvo[:, tok0:tok0 + L], op0=ALU.mult, op1=ALU.add), reads=[ubb, ppb, cvob], writes=[cvob])
                        P.op("dve", lambda e: e.scalar_tensor_tensor(out=cvo[:, tok0:tok0 + L], in0=ub[:, p0 + 1:p0 + 1 + L], scalar=ppt[:, o + 24 + ch:o + 24 + ch + 1],
                                                                     in1=cvo[:, tok0:tok0 + L], op0=ALU.mult, op1=ALU.add), reads=[ubb, ppb, cvob], writes=[cvob])
                    part, cc = divmod(ch, 4)
                    for t4 in range(T // 512):
                        bank, bb = nb()
                        for j in range(4):
                            tk = t4 * 512 + j * 128
                            P.op("pe", lambda e: e.transpose(bank[:, j * 128:(j + 1) * 128], cvo[:, tk:tk + 128], idf[:]), reads=[cvob, idfb], writes=[bb], inc=(j == 3))
                        ts_, tsb = tst[ti % 3]
                        ti += 1
                        if ti % 2 == 0:
                            P.op("act", lambda e: e.activation(out=ts_[:], in_=bank.rearrange("p (a b) -> p a b", a=4), func=AF.Copy), reads=[bb], writes=[tsb])
                        else:
                            P.op("dve", lambda e: e.tensor_copy(out=ts_[:], in_=bank.rearrange("p (a b) -> p a b", a=4)), reads=[bb], writes=[tsb])
                        P.dma("act", X3t[part, t4 * 512:(t4 + 1) * 512, cc * 128:(cc + 1) * 128].rearrange("(a p) c -> p a c", p=128), ts_[:], reads=[tsb])
                P.barrier()
            with contextlib.ExitStack() as es2:
                ubuf = [sb(f"pb{i}", [128, NPS + 1, 8 + LS + 8], F32) for i in range(1)]
                a1, a1b = sb("pa1", [128, 8 + LS + 8], F32)
                a2, a2b = sb("pa2", [128, 8 + LS + 8], F32)
                pl, plb = sb("ppl", [128, LS], F32)
                plb16 = [sb(f"plb{i}", [128, LS], BF16) for i in range(1)]
                rcs_t, rcsb = sb("rcs", [128, LS], F32)
                wp, wpb = sb("wp", [128, 128], F32)
                wpbf, wpbfb = sb("wpbf", [128, 128], BF16)
                pos_, posb = sb("pos", [128, 512], BF16)
                seqs = [(0, LS, "rcs")] + [(LS + s * LP, LP, "rcp") for s in range(NPS)]
                for g, w in enumerate((2, 4, 8, 16)):
                    ub, ubb = ubuf[0]
                    P.op("pool", lambda e: e.memset(ub[:], 0.0), writes=[ubb])
                    P.dma("sp", wp[:], w_pool[l, g], writes=[wpb])
                    P.op("dve", lambda e: e.tensor_copy(out=wpbf[:], in_=wp[:]), reads=[wpb], writes=[wpbfb])
                    for si, (tok0, L, rcn) in enumerate(seqs):
                        P.dma("sp", ub[:, si, 8:8 + L], U[1536 + g * 128:1536 + (g + 1) * 128, tok0:tok0 + L], writes=[ubb])
                    for si, (tok0, L, rcn) in enumerate(seqs):
                        W = 8 + L + 8
                        P.dma("sp", rcs_t[:, 0:L], cd[rcn][:, g, :], writes=[rcsb])
                        src = ub[:, si, :]
                        cur, curb = None, None
                        lv = 1
                        bufs = [(a1, a1b), (a2, a2b)]
                        bi = 0
                        prev, prevb = src, ubb
                        while lv < w:
                            dst_, dstb = bufs[bi % 2]
                            bi += 1
                            n = W - lv
                            P.op("pool", lambda e: e.tensor_tensor(out=dst_[:, 0:n], in0=prev[:, 0:n], in1=prev[:, lv:lv + n], op=ALU.add),
                                 reads=[prevb], writes=[dstb])
                            prev, prevb = dst_, dstb
                            lv *= 2
                        s0 = 8 - w // 2
                        P.op("dve", lambda e: e.tensor_tensor(out=pl[:, 0:L], in0=prev[:, s0:s0 + L], in1=rcs_t[:, 0:L], op=ALU.mult), reads=[prevb, rcsb], writes=[plb])
                        pb16, pb16b = plb16[0]
                        P.op("dve", lambda e: e.tensor_tensor(out=pb16[:, 0:L], in0=pl[:, 0:L], in1=ub[:, si, 8:8 + L], op=ALU.subtract), reads=[plb, ubb], writes=[pb16b])
                        o, _ = PP[f"psc{l}"]
                        for c0 in range(0, L, 512):
                            n = min(512, L - c0)
                            bank, bb = nb()
                            P.op("pe", lambda e: e.matmul(bank[:, 0:n], lhsT=wpbf[:], rhs=pb16[:, c0:c0 + n], start=True, stop=True), reads=[wpbfb, pb16b], writes=[bb])
                            P.op("act", lambda e: e.activation(out=pos_[:, 0:n], in_=bank[:, 0:n], func=AF.Identity, scale=ppt[:, o + g:o + g + 1]), reads=[bb, ppb], writes=[posb])
                            P.dma("act", mixT[1536 + g * 128:1536 + (g + 1) * 128, tok0 + c0:tok0 + c0 + n], pos_[:, 0:n], reads=[posb])
                P.barrier()

        def phase_C2(l, L, nseq, tokbase, tck, tsk, gnk, zfk, ntlk, Kf):
            nonlocal es2
            NT = L // 128
            NF = (L + 1 + 127) // 128
            NB = nseq * 512
            Tc = cd[tck]
            Ts = cd[tsk]
            with contextlib.ExitStack() as es_outer:
                es2 = es_outer
                h2, h2b = sb("h2", [64, L], BF16)
                w3, w3b = sb("w3", [64, 2048], F32)
                w3h, w3hb = sb("w3h", [64, 2048], BF16)
                fp_, fpb = sb("fp", [64, 8], F32)
                P.dma("sp", w3[:], hy_w3[l], writes=[w3b])
                P.op("dve", lambda e: e.tensor_copy(out=w3h[:], in_=w3[:]), reads=[w3b], writes=[w3hb])
                for li, (fn_, bn_) in enumerate(((f"hf1{l}", f"hb1{l}"), (f"hf2{l}", f"hb2{l}"))):
                    of, _ = PP[fn_]
                    obb, _ = PP[bn_]
                    c4 = li * 4
                    P.op("dve", lambda e: e.tensor_scalar(out=fp_[:, c4:c4 + 1], in0=ppt[0:64, of:of + 1], scalar1=0.5, scalar2=None, op0=ALU.mult), reads=[ppb], writes=[fpb])
                    P.op("dve", lambda e: e.tensor_tensor(out=fp_[:, c4 + 1:c4 + 2], in0=fp_[:, c4:c4 + 1], in1=ppt[0:64, obb:obb + 1], op=ALU.mult), reads=[ppb, fpb], writes=[fpb])
                    P.op("dve", lambda e: e.tensor_scalar(out=fp_[:, c4 + 2:c4 + 4], in0=fp_[:, c4:c4 + 2], scalar1=0.5, scalar2=None, op0=ALU.mult), reads=[fpb], writes=[fpb])
                with contextlib.ExitStack() as es_inner:
                    es2 = es_inner
                    zf, zfb = sb("zf", [33, L], F32)
                    w1, w1b = sb("w1", [33, 64], F32)
                    w2, w2b = sb("w2", [64, 64], F32)
                    h1, h1b = sb("h1", [64, L], F32)
                    s2, s2b = sb("s2", [64, 512], F32)
                    s4, s4b = sb("s4", [64, 512], F32)
                    P.dma("sp", zf[:], cd[zfk][:, :], writes=[zfb])
                    P.dma("sp", w1[:], hy_w1[l], writes=[w1b])
                    P.dma("sp", w2[:], hy_w2[l], writes=[w2b])

                    def sin_layer(li, wt, wtb_, K, src, srcb, dst, dstb):
                        c4 = li * 4
                        for c0 in range(0, L, 512):
                            n = min(512, L - c0)
                            bank, bb = nb()
                            P.op("pe", lambda e: e.matmul(bank[0:64, 0:n], lhsT=wt[0:K, :], rhs=src[0:K, c0:c0 + n], start=True, stop=True), reads=[wtb_, srcb], writes=[bb])
                            P.op("act", lambda e: e.activation(out=s2[:, 0:n], in_=bank[0:64, 0:n], func=AF.Sin, scale=fp_[:, c4:c4 + 1], bias=fp_[:, c4 + 1:c4 + 2]), reads=[bb, fpb], writes=[s2b])
                            P.op("act", lambda e: e.activation(out=s4[:, 0:n], in_=bank[0:64, 0:n], func=AF.Sin, scale=fp_[:, c4 + 2:c4 + 3], bias=fp_[:, c4 + 3:c4 + 4]), reads=[bb, fpb], writes=[s4b])
                            P.op("dve", lambda e: e.tensor_tensor(out=s4[:, 0:n], in0=s4[:, 0:n], in1=s4[:, 0:n], op=ALU.mult), reads=[s4b], writes=[s4b])
                            P.op("dve", lambda e: e.tensor_scalar(out=s4[:, 0:n], in0=s4[:, 0:n], scalar1=-4.0, scalar2=2.0, op0=ALU.mult, op1=ALU.add), reads=[s4b], writes=[s4b])
                            P.op("dve", lambda e: e.tensor_tensor(out=dst[:, c0:c0 + n], in0=s4[:, 0:n], in1=s2[:, 0:n], op=ALU.mult), reads=[s4b, s2b], writes=[dstb])

                    sin_layer(0, w1, w1b, 33, zf, zfb, h1, h1b)
                    sin_layer(1, w2, w2b, 64, h1, h1b, h2, h2b)
                    P.barrier()
                es2 = es_outer
                gn, gnb = sb("gn", [128, NF], F32)
                ntl, ntlb = sb("ntl", [128, NT], F32)
                dlt, dltb = sb("dlt", [128, 512], F32)
                ksd, ksdb = sb("ksd", [128, NT, 2, 512], BF16)
                win, winb = sb("win", [128, 512], F32)
                wf = [sb(f"wf{i}", [128, 512], F32) for i in range(2)]
                wab = [sb(f"wab{i}", [128, 512], BF16) for i in range(2)]
                rn, rnb = sb("rn", [128, 512], F32)
                tsl = [sb(f"tsl{i}", [128, 2, NT * 128], BF16) for i in range(2)]
                kfo = [sb(f"kfo{i}", [128, 2, 512], BF16) for i in range(2)]
                P.dma("sp", gn[:], cd[gnk][:, :], writes=[gnb])
                P.dma("sp", ntl[:], cd[ntlk][:, :], writes=[ntlb])
                P.dma("sp", dlt[:], cd["delta"][:, :], writes=[dltb])
                ki = 0
                for o_ in range(2):
                    nacc = hold(1)
                    for tc in range(NT):
                        P.op("act", lambda e: e.activation(out=win[:], in_=dlt[:], func=AF.Exp, scale=ntl[:, tc:tc + 1]), reads=[dltb, ntlb], writes=[winb])
                        for dr in range(2):
                            bank, bb = nb()
                            cb = dr * 1024 + o_ * 512
                            P.op("pe", lambda e: e.matmul(bank, lhsT=h2[:, tc * 128:(tc + 1) * 128], rhs=w3h[:, cb:cb + 512], start=True, stop=True), reads=[h2b, w3hb], writes=[bb])
                            wt_, wtb_ = wf[dr]
                            P.op("dve", lambda e: e.scalar_tensor_tensor(out=wt_[:], in0=win[:], scalar=0.05, in1=bank, op0=ALU.add, op1=ALU.mult), reads=[winb, bb], writes=[wtb_])
                            if dr == 1 and tc == 0:
                                P.op("dve", lambda e: e.memset(wt_[0:1, :], 0.0), writes=[wtb_])
                            ab_, abb_ = wab[dr]
                            P.op("act", lambda e: e.activation(out=ab_[:], in_=wt_[:], func=AF.Abs), reads=[wtb_], writes=[abb_])
                            P.op("pe", lambda e: e.matmul(nacc[0][0], lhsT=oneb[:], rhs=ab_[:], start=(tc == 0 and dr == 0), stop=(tc == NT - 1 and dr == 1)),
                                 reads=[abb_, onebb], writes=[nacc[0][1]], inc=True)
                        P.op("dve", lambda e: e.tensor_tensor(out=ksd[:, tc, 0, :], in0=wf[0][0][:], in1=wf[1][0][:], op=ALU.add), reads=[wf[0][1], wf[1][1]], writes=[ksdb])
                        P.op("pool", lambda e: e.tensor_tensor(out=ksd[:, tc, 1, :], in0=wf[0][0][:], in1=wf[1][0][:], op=ALU.subtract), reads=[wf[0][1], wf[1][1]], writes=[ksdb])
                    P.op("dve", lambda e: e.reciprocal(out=rn[:], in_=nacc[0][0]), reads=[nacc[0][1]], writes=[rnb])
                    release(nacc)
                    for fc in range(NF):
                        tl_, tlb_ = tsl[fc % 2]
                        P.dma("sp", tl_[:, 0, :], Tc[fc, :, 0:NT, :].rearrange("p a b -> p (a b)"), writes=[tlb_])
                        P.dma("sp", tl_[:, 1, :], Ts[fc, :, 0:NT, :].rearrange("p a b -> p (a b)"), writes=[tlb_])
                        ko, kob = kfo[ki % 2]
                        ki += 1
                        for cs in range(2):
                            bank, bb = nb()
                            for tc in range(NT):
                                P.op("pe", lambda e: e.matmul(bank, lhsT=tl_[:, cs, tc * 128:(tc + 1) * 128], rhs=ksd[:, tc, cs, :], start=(tc == 0), stop=(tc == NT - 1)),
                                     reads=[tlb_, ksdb], writes=[bb], inc=(tc == NT - 1))
                            P.op("dve", lambda e: e.scalar_tensor_tensor(out=ko[:, cs, :], in0=bank, scalar=gn[:, fc:fc + 1], in1=rn[:], op0=ALU.mult, op1=ALU.mult),
                                 reads=[bb, gnb, rnb], writes=[kob])
                        P.dma("act", Kf[o_, fc], ko[:], reads=[kob])
                P.barrier()
            with contextlib.ExitStack() as es2:
                zt, ztb = sb("zt", [128, NT, NB], BF16)
                Y, Yb = sb("Y", [128, NF, 2, NB], BF16)
                tsl = [sb(f"tsd{i}", [128, 2, NF * 128], BF16) for i in range(2)]
                kfl = [sb(f"kfl{i}", [128, 2, 512], BF16) for i in range(2)]
                zr, zrb = sb("zr", [128, 2, 512], F32)
                ta, tab = sb("ta", [128, 512], F32)
                tb_, tbb = sb("tb", [128, 512], F32)
                xg_ = [sb(f"xgt{i}", [128, 512], BF16) for i in range(2)]
                ga, gab = sb("ga", [128, 512], F32)
                hyo = [sb(f"hyo{i}", [128, 512], BF16) for i in range(2)]
                hts = [sb(f"hts{i}", [128, 4, 128], BF16) for i in range(2)]
                for s_ in range(nseq):
                    tk0 = tokbase + s_ * L
                    P.dma("sp", zt[:, :, s_ * 512:(s_ + 1) * 512], X3t[0, tk0:tk0 + L, :].rearrange("(a p) c -> p a c", p=128), writes=[ztb])
                od, _ = PP[f"hydrep{l}"]
                si_ = 0
                for o_ in range(2):
                    for fc in range(NF):
                        tl_, tlb_ = tsl[si_ % 2]
                        si_ += 1
                        P.dma("sp", tl_[:, 0, 0:NT * 128], Tc[fc, :, 0:NT, :].rearrange("p a b -> p (a b)"), writes=[tlb_])
                        P.dma("sp", tl_[:, 1, 0:NT * 128], Ts[fc, :, 0:NT, :].rearrange("p a b -> p (a b)"), writes=[tlb_])
                        kf_, kfb_ = kfl[fc % 2]
                        P.dma("sp", kf_[:], Kf[o_, fc], writes=[kfb_])
                        for s_ in range(nseq):
                            bks = []
                            for cs in range(2):
                                bank, bb = nb()
                                bks.append((bank, bb))
                                for tc in range(NT):
                                    P.op("pe", lambda e: e.matmul(bank, lhsT=tl_[:, cs, tc * 128:(tc + 1) * 128], rhs=zt[:, tc, s_ * 512:(s_ + 1) * 512], start=(tc == 0), stop=(tc == NT - 1)),
                                         reads=[tlb_, ztb], writes=[bb], inc=(tc == NT - 1))
                            P.op("act", lambda e: e.activation(out=zr[:, 0, :], in_=bks[0][0], func=AF.Copy), reads=[bks[0][1]], writes=[zrb])
                            P.op("act", lambda e: e.activation(out=zr[:, 1, :], in_=bks[1][0], func=AF.Copy), reads=[bks[1][1]], writes=[zrb])
                            P.op("dve", lambda e: e.tensor_tensor(out=ta[:], in0=zr[:, 0, :], in1=kf_[:, 0, :], op=ALU.mult), reads=[zrb, kfb_], writes=[tab])
                            P.op("pool", lambda e: e.tensor_tensor(out=tb_[:], in0=zr[:, 1, :], in1=kf_[:, 1, :], op=ALU.mult), reads=[zrb, kfb_], writes=[tbb])
                            P.op("dve", lambda e: e.tensor_tensor(out=Y[:, fc, 0, s_ * 512:(s_ + 1) * 512], in0=ta[:], in1=tb_[:], op=ALU.subtract), reads=[tab, tbb], writes=[Yb])
                            P.op("dve", lambda e: e.tensor_tensor(out=ta[:], in0=zr[:, 0, :], in1=kf_[:, 1, :], op=ALU.mult), reads=[zrb, kfb_], writes=[tab])
                            P.op("pool", lambda e: e.tensor_tensor(out=tb_[:], in0=zr[:, 1, :], in1=kf_[:, 0, :], op=ALU.mult), reads=[zrb, kfb_], writes=[tbb])
                            P.op("dve", lambda e: e.tensor_tensor(out=Y[:, fc, 1, s_ * 512:(s_ + 1) * 512], in0=ta[:], in1=tb_[:], op=ALU.add), reads=[tab, tbb], writes=[Yb])
                    for tc in range(NT):
                        tl_, tlb_ = tsl[si_ % 2]
                        si_ += 1
                        P.dma("sp", tl_[:, 0, :], Tc[tc, :, 0:NF, :].rearrange("p a b -> p (a b)"), writes=[tlb_])
                        P.dma("sp", tl_[:, 1, :], Ts[tc, :, 0:NF, :].rearrange("p a b -> p (a b)"), writes=[tlb_])
                        for s_ in range(nseq):
                            tk = tokbase + s_ * L + tc * 128
                            xg, xgb_ = xg_[(tc * nseq + s_) % 2]
                            P.dma("sp", xg[:], X3t[1 + o_, tk:tk + 128, :], writes=[xgb_])
                            bank, bb = nb()
                            n_mm = 2 * NF
                            i_mm = 0
                            for fc in range(NF):
                                for cs in range(2):
                                    P.op("pe", lambda e: e.matmul(bank, lhsT=tl_[:, cs, fc * 128:(fc + 1) * 128], rhs=Y[:, fc, cs, s_ * 512:(s_ + 1) * 512],
                                                                  start=(i_mm == 0), stop=(i_mm == n_mm - 1)),
                                         reads=[tlb_, Yb], writes=[bb], inc=(i_mm == n_mm - 1))
                                    i_mm += 1
                            zsl = zt[:, tc, s_ * 512:(s_ + 1) * 512]
                            P.op("dve", lambda e: e.tensor_tensor(out=ga[:], in0=zsl, in1=ppt[:, od + o_ * 512:od + (o_ + 1) * 512], op=ALU.mult), reads=[ztb, ppb], writes=[gab])
                            P.op("dve", lambda e: e.tensor_tensor(out=ga[:], in0=ga[:], in1=bank, op=ALU.add), reads=[gab, bb], writes=[gab])
                            if o_ == 0:
                                P.op("dve", lambda e: e.tensor_tensor(out=zsl, in0=ga[:], in1=xg[:], op=ALU.mult), reads=[gab, xgb_], writes=[ztb])
                            else:
                                ho, hob = hyo[(tc * nseq + s_) % 2]
                                P.op("dve", lambda e: e.tensor_tensor(out=ho[:], in0=ga[:], in1=xg[:], op=ALU.mult), reads=[gab, xgb_], writes=[hob])
                                bk2, bb2 = nb()
                                bk2b = bk2.bitcast(BF16)
                                for j in range(4):
                                    P.op("pe", lambda e: e.transpose(bk2b[:, j * 128:(j + 1) * 128], ho[:, j * 128:(j + 1) * 128], idb[:]), reads=[hob, idbb], writes=[bb2], inc=(j == 3))
                                ht, htb = hts[(tc * nseq + s_) % 2]
                                P.op("act", lambda e: e.activation(out=ht[:], in_=bk2b[:, 0:512].rearrange("p (a b) -> p a b", a=4), func=AF.Copy), reads=[bb2], writes=[htb])
                                P.dma("act", mixT[1024:1536, tk:tk + 128].rearrange("(a p) t -> p a t", p=128), ht[:], reads=[htb])
                P.barrier()

        def phase_D(l, last):
            nonlocal es2
            for nm in ("out", "gate", "up", "down"):
                wait_conv(nm, l)
            with contextlib.ExitStack() as es2:
                xgs = [sb(f"xd{i}", [128, KC, G], F32) for i in range(1)]
                xn, xnb = sb("xn", [128, KC, G], BF16)
                mx, mxb = sb("mx", [128, KC, G], BF16)
                hT, hTb = sb("h2T", [128, KC, G], BF16)
                rst, rstb = sb("rstd", [128, G], F32)
                actT, actb = sb("actT", [128, HC, G], BF16)
                wsl = [sb(f"wd{i}", [128, KC, 512], BF16) for i in range(3)]
                sg = [sb(f"sg{i}", [128, G], F32) for i in range(2)]
                yst = [sb(f"yst{i}", [128, 512], F32) for i in range(2)]
                wi = 0
                gi = 0
                xgq = [Buf(f"xgq{i}") for i in range(4)]
                for g in range(NG):
                    tok0 = g * G
                    cond = 1 if g < 8 else 0
                    xg, xgb = xgs[0]
                    for cg_ in range(4):
                        P.dma("sp", xg[:, cg_ * 4:(cg_ + 1) * 4, :], xT.rearrange("(kc p) t -> p kc t", p=128)[:, cg_ * 4:(cg_ + 1) * 4, tok0:tok0 + G], writes=[xgq[cg_]])
                    P.dma("sp", mx[:], mixT.rearrange("(kc p) t -> p kc t", p=128)[:, :, tok0:tok0 + G], writes=[mxb])
                    for s in range(4):
                        wt, wtb = wsl[wi % 3]
                        wi += 1
                        P.dma("sp", wt[:], wb_out[l].rearrange("(kc p) n -> p kc n", p=128)[:, :, s * 512:(s + 1) * 512], writes=[wtb])
                        for j in range(4):
                            cc = s * 4 + j
                            bank, bb = nb()
                            for kc in range(KC):
                                P.op("pe", lambda e: e.matmul(bank, lhsT=wt[:, kc, j * 128:(j + 1) * 128], rhs=mx[:, kc, :], start=(kc == 0), stop=(kc == KC - 1)),
                                     reads=[wtb, mxb], writes=[bb], inc=(kc == KC - 1))
                            P.op("dve", lambda e: e.scalar_tensor_tensor(out=xg[:, cc, :], in0=bank, scalar=modt[:, l, 2 * 16 + cc, cond:cond + 1], in1=xg[:, cc, :],
                                                                         op0=ALU.mult, op1=ALU.add), reads=[bb, modb, xgq[cc // 4]], writes=[xgq[cc // 4]])
                    norm_mod(xg, xgq, xn, xnb, hT, hTb, hT, hTb, rst, rstb, l, 1, cond)
                    ring = wsl + [(mx, mxb)]
                    ri = 0
                    for s in range(11):
                        wg, wgb = ring[ri % 4]
                        ri += 1
                        wu, wub = ring[ri % 4]
                        ri += 1
                        P.dma("sp", wg[:], wb_gate[l].rearrange("(kc p) n -> p kc n", p=128)[:, :, s * 512:(s + 1) * 512], writes=[wgb])
                        P.dma("sp", wu[:], wb_up[l].rearrange("(kc p) n -> p kc n", p=128)[:, :, s * 512:(s + 1) * 512], writes=[wub])
                        for j in range(4):
                            hc = s * 4 + j
                            bg, bgb = nb()
                            for kc in range(KC):
                                P.op("pe", lambda e: e.matmul(bg, lhsT=wg[:, kc, j * 128:(j + 1) * 128], rhs=hT[:, kc, :], start=(kc == 0), stop=(kc == KC - 1)),
                                     reads=[wgb, hTb], writes=[bgb], inc=(kc == KC - 1))
                            bu, bub = nb()
                            for kc in range(KC):
                                P.op("pe", lambda e: e.matmul(bu, lhsT=wu[:, kc, j * 128:(j + 1) * 128], rhs=hT[:, kc, :], start=(kc == 0), stop=(kc == KC - 1)),
                                     reads=[wub, hTb], writes=[bub], inc=(kc == KC - 1))
                            sg_, sgb = sg[gi % 2]
                            gi += 1
                            P.op("act", lambda e: e.activation(out=sg_[:], in_=bg, func=AF.Silu), reads=[bgb], writes=[sgb])
                            P.op("dve", lambda e: e.tensor_tensor(out=actT[:, hc, :], in0=sg_[:], in1=bu, op=ALU.mult), reads=[sgb, bub], writes=[actb])
                    for cg in range(4):
                        banks = hold(4)
                        for hq in range(4):
                            wt, wtb = ring[ri % 4]
                            ri += 1
                            P.dma("sp", wt[:, 0:11, :], wb_down[l].rearrange("(hc p) n -> p hc n", p=128)[:, hq * 11:(hq + 1) * 11, cg * 512:(cg + 1) * 512], writes=[wtb])
                            for j in range(4):
                                for hl in range(11):
                                    fst = (hq == 0 and hl == 0)
                                    lst = (hq == 3 and hl == 10)
                                    P.op("pe", lambda e: e.matmul(banks[j][0], lhsT=wt[:, hl, j * 128:(j + 1) * 128], rhs=actT[:, hq * 11 + hl, :], start=fst, stop=lst),
                                         reads=[wtb, actb], writes=[banks[j][1]], inc=(hl == 10))
                        for j in range(4):
                            cc = cg * 4 + j
                            P.op("dve", lambda e: e.scalar_tensor_tensor(out=xg[:, cc, :], in0=banks[j][0], scalar=modt[:, l, 5 * 16 + cc, cond:cond + 1], in1=xg[:, cc, :],
                                                                         op0=ALU.mult, op1=ALU.add), reads=[banks[j][1], modb, xgq[cg]], writes=[xgq[cg]])
                        release(banks)
                        if not last:
                            P.dma("act", xT.rearrange("(kc p) t -> p kc t", p=128)[:, cg * 4:(cg + 1) * 4, tok0:tok0 + G], xg[:, cg * 4:(cg + 1) * 4, :], reads=[xgq[cg]])
                    if not last:
                        pass
                    else:
                        for tt in range(4):
                            for k4 in range(4):
                                ys_, ysb = yst[k4 % 2]
                                bank, bb = nb()
                                for j in range(4):
                                    kc = k4 * 4 + j
                                    P.op("pe", lambda e: e.transpose(bank[:, j * 128:(j + 1) * 128], xg[:, kc, tt * 128:(tt + 1) * 128], idf[:]), reads=[xgq[k4], idfb], writes=[bb], inc=(j == 3))
                                if k4 % 2 == 0:
                                    P.op("act", lambda e: e.activation(out=ys_[:], in_=bank, func=AF.Copy), reads=[bb], writes=[ysb])
                                else:
                                    P.op("dve", lambda e: e.tensor_copy(out=ys_[:], in_=bank), reads=[bb], writes=[ysb])
                                tok = tok0 + tt * 128
                                dst = ys[tok:tok + 128, k4 * 512:(k4 + 1) * 512] if tok < LS else yp[tok - LS:tok - LS + 128, k4 * 512:(k4 + 1) * 512]
                                P.dma("act", dst, ys_[:], reads=[ysb])
                P.barrier()

        phase_mod(0)
        if stop_after == ("mod", 0):
            stages = []
        phase_xT()
        stages = []
        if stop_after not in (("mod", 0), ("xT", 0)):
            for l in range(nlayers):
                stages += [("W", l), ("A", l), ("B", l), ("C1", l), ("C2s", l), ("C2p", l), ("D", l)]
        for (st, l) in stages:
            if st == "W":
                if l == 0:
                    phase_W(l, ["in"])
            elif st == "A":
                phase_A(l)
                if l + 1 < nlayers:
                    phase_mod(l + 1)
            elif st == "B":
                phase_B(l)
            elif st == "C1":
                phase_C1(l)
            elif st == "C2s":
                phase_C2(l, LS, 1, 0, "tcs", "tss", "gns", "zfs", "ntls", Kfs)
            elif st == "C2p":
                phase_C2(l, LP, NPS, LS, "tcp", "tsp", "gnp", "zfp", "ntlp", Kfp)
            elif st == "D":
                phase_D(l, last=(l == nlayers - 1))
            if stop_after == (st, l):
                break
        P.barrier()
    return nc


_W_KEYS = ("w_ada", "w_in", "w_out", "w_gate", "w_up", "w_down", "hy_w1", "hy_w2", "hy_w3", "w_pool")


def make_in_maps(inp, cores):
    cst = _constants()
    maps = []
    wts = {k: np.ascontiguousarray(np.asarray(inp[k], np.float32)) for k in _W_KEYS}
    for i in cores:
        m = {}
        m["xs"] = np.ascontiguousarray(np.asarray(inp["x_sample"][i], np.float32))
        m["xp"] = np.ascontiguousarray(np.asarray(inp["x_prompt"][4 * i:4 * i + 4], np.float32).reshape(NPS * LP, D))
        m["ck"] = np.ascontiguousarray(np.asarray(inp["cache_k"][i], np.float32).reshape(DEPTH, LP, 1024))
        m["cv"] = np.ascontiguousarray(np.asarray(inp["cache_v"][i], np.float32).reshape(DEPTH, LP, 1024))
        m["pp"] = _pack_params(inp, i)
        m.update(wts)
        for k, v in cst.items():
            m["c_" + k] = v
        maps.append(m)
    return maps


def kernel(**inputs):
    inp = {k: np.asarray(v) for k, v in inputs.items()}
    nc = build_program()
    maps = make_in_maps(inp, list(range(NCORES)))
    res = run_bass_kernel_spmd(nc, maps, core_ids=list(range(NCORES)))
    r = res.results
    y_s = np.stack([np.asarray(r[i]["ys"], np.float32) for i in range(NCORES)], 0)
    y_p = np.concatenate([np.asarray(r[i]["yp"], np.float32).reshape(NPS, LP, D) for i in range(NCORES)], 0)
    s_k = np.concatenate([np.asarray(r[i]["sk"], np.float32).reshape(NPS, DEPTH, LP, H, 2, 64) for i in range(NCORES)], 0)
    s_v = np.concatenate([np.asarray(r[i]["sv"], np.float32).reshape(NPS, DEPTH, LP, H, 128) for i in range(NCORES)], 0)
    return (y_p, y_s, s_k, s_v)
```

```python
import math
import numpy as np
import ml_dtypes
import concourse.bass as bass
import concourse.mybir as mybir
from concourse.bass_utils import run_bass_kernel_spmd

F32 = mybir.dt.float32
BF16 = mybir.dt.bfloat16
AF = mybir.ActivationFunctionType
ALU = mybir.AluOpType
AX = mybir.AxisListType

D = 2048
KC = 16
DEPTH = 2
LS = 4096
LP = 256
NPS = 4
T = LS + NPS * LP
G = 512
NG = T // G
H = 8
FFN = 5632
HC = FFN // 128
INC = 5120
EPS = 1e-6
NCORES = 8


def _dft_tables(L):
    N = 2 * L
    nch = (L + 1 + 127) // 128
    a = np.arange(nch * 128, dtype=np.int64)
    prod = (a[:, None] * a[None, :]) % N
    ang = 2.0 * np.pi * prod.astype(np.float64) / N
    c = np.cos(ang)
    s = np.sin(ang)
    def tile(m):
        m4 = m.reshape(nch, 128, nch, 128)
        return np.ascontiguousarray(m4.transpose(2, 1, 0, 3)).astype(ml_dtypes.bfloat16)
    f = np.arange(nch * 128)
    g = np.where(f == 0, 1.0, np.where(f < L, 2.0, np.where(f == L, 1.0, 0.0))) / N
    gN = np.ascontiguousarray(g.reshape(nch, 128).T).astype(np.float32)
    return tile(c), tile(s), gN, nch


def _zfeat(L):
    t = np.linspace(0.0, 1.0, L, dtype=np.float32)[:, None]
    wpos = (2.0 * math.pi * np.arange(L, dtype=np.float32)[:, None] / L).astype(np.float32)
    bands = np.linspace(1e-4, 15, 16, dtype=np.float32)[None, :]
    z = np.concatenate([t, np.cos(bands * wpos), np.sin(bands * wpos)], -1).astype(np.float32)
    ntl = np.ascontiguousarray((-t[:, 0]).reshape(-1, 128).T).astype(np.float32)
    return np.ascontiguousarray(z.T), ntl


def _rcount(L):
    t = np.arange(L)
    out = np.zeros((4, L), np.float32)
    for g, w in enumerate((2, 4, 8, 16)):
        lo = np.clip(t - w // 2, 0, L)
        hi = np.clip(t - w // 2 + w, 0, L)
        out[g] = 1.0 / (hi - lo)
    return out


_CONST = None


def _constants():
    global _CONST
    if _CONST is not None:
        return _CONST
    c = {}
    c["ident_f"] = np.eye(128, dtype=np.float32)
    c["ident_b"] = np.eye(128, dtype=np.float32).astype(ml_dtypes.bfloat16)
    c["ones_b"] = np.ones((128, 128), np.float32).astype(ml_dtypes.bfloat16)
    bo = np.zeros((128, 128), np.float32)
    bo[:64, :64] = 1.0
    bo[64:, 64:] = 1.0
    c["bones_b"] = bo.astype(ml_dtypes.bfloat16)
    rp = np.zeros((128, 128), np.float32)
    for m in range(128):
        if m % 64 < 32:
            rp[m + 32, m] = -1.0
        else:
            rp[m - 32, m] = 1.0
    c["rperm_f"] = rp
    c["rperm_b"] = rp.astype(ml_dtypes.bfloat16)
    tt = np.arange(LS)
    rows = (tt // 64).astype(np.float32)
    cols = (tt % 64).astype(np.float32)
    inv = (10000.0 ** (-np.arange(16, dtype=np.float32) / 16)).astype(np.float32)
    ang = np.concatenate([rows[:, None] * inv, cols[:, None] * inv], -1)
    idx = np.arange(128) % 32
    c["ropec"] = np.ascontiguousarray(np.cos(ang)[:, idx].T).astype(np.float32)
    c["ropes"] = np.ascontiguousarray(np.sin(ang)[:, idx].T).astype(np.float32)
    c["tcs"], c["tss"], c["gns"], _ = _dft_tables(LS)
    c["tcp"], c["tsp"], c["gnp"], _ = _dft_tables(LP)
    c["zfs"], c["ntls"] = _zfeat(LS)
    c["zfp"], c["ntlp"] = _zfeat(LP)
    mn = math.log(1e-2) / 1.5
    mx = math.log(1e-2) / 0.3
    dl = np.abs(np.linspace(mn, mx, 512, dtype=np.float32))
    c["delta"] = np.ascontiguousarray(np.broadcast_to(dl[None, :], (128, 512))).astype(np.float32)
    c["rcs"] = np.ascontiguousarray(np.broadcast_to(_rcount(LS)[None], (128, 4, LS))).astype(np.float32)
    c["rcp"] = np.ascontiguousarray(np.broadcast_to(_rcount(LP)[None], (128, 4, LP))).astype(np.float32)
    _CONST = c
    return c


PP = {}
_off = 0


def _pp(name, n):
    global _off
    PP[name] = (_off, n)
    _off += n


for _l in range(DEPTH):
    _pp(f"bada{_l}", 96)
    _pp(f"n1{_l}", 16)
    _pp(f"n2{_l}", 16)
    _pp(f"qg{_l}", 1)
    _pp(f"kg{_l}", 1)
    _pp(f"wsc{_l}", 36)
    _pp(f"bsc{_l}", 12)
    _pp(f"hyd{_l}", 8)
    _pp(f"psc{_l}", 4)
    _pp(f"hb1{_l}", 1)
    _pp(f"hf1{_l}", 1)
    _pp(f"hb2{_l}", 1)
    _pp(f"hf2{_l}", 1)
    _pp(f"wlam{_l}", 256)
    _pp(f"subln{_l}", 128)
    _pp(f"hydrep{_l}", 1024)
_pp("cvT", 32)
NPP = _off


def _pack_params(inp, core):
    pp = np.zeros((128, NPP), np.float32)

    def put(name, arr):
        o, n = PP[name]
        pp[:, o:o + n] = np.asarray(arr, np.float32).reshape(128, n)

    def cm(v, nch):
        return np.asarray(v, np.float32).reshape(nch, 128).T

    for l in range(DEPTH):
        put(f"bada{l}", cm(inp["b_ada"][l], 96))
        put(f"n1{l}", cm(inp["norm1"][l], 16))
        put(f"n2{l}", cm(inp["norm2"][l], 16))
        put(f"qg{l}", np.tile(inp["q_norm"][l], 2)[:, None])
        put(f"kg{l}", np.tile(inp["k_norm"][l], 2)[:, None])
        put(f"wsc{l}", np.concatenate([cm(inp["w_sc"][l][j], 12) for j in range(3)], 1))
        put(f"bsc{l}", cm(inp["b_sc"][l], 12))
        put(f"hyd{l}", np.concatenate([cm(inp["hy_d"][l][o], 4) for o in range(2)], 1))
        put(f"psc{l}", cm(inp["pool_scale"][l], 4))
        for nm, key in (("hb1", "hy_b1"), ("hf1", "hy_fr1"), ("hb2", "hy_b2"), ("hf2", "hy_fr2")):
            v = np.zeros(128, np.float32)
            v[:64] = inp[key][l]
            put(f"{nm}{l}", v[:, None])
        put(f"wlam{l}", np.broadcast_to(np.asarray(inp["w_lambda"][l]).reshape(1, 256), (128, 256)))
        put(f"subln{l}", np.broadcast_to(np.asarray(inp["subln"][l]).reshape(1, 128), (128, 128)))
        put(f"hydrep{l}", np.broadcast_to(np.asarray(inp["hy_d"][l]).reshape(1, 1024), (128, 1024)))
    cv = np.stack([np.asarray(inp["c_ctx"]), np.asarray(inp["c"][core])], 0)
    cvT = cv.reshape(2, 16, 128).transpose(2, 1, 0).reshape(128, 32)
    put("cvT", cvT)
    return pp


class Buf:
    __slots__ = ("name", "w", "r")

    def __init__(self, name):
        self.name = name
        self.w = None
        self.r = {}


class EngQ:
    def __init__(self, name, eng, key):
        self.name = name
        self.eng = eng
        self.key = key
        self.cnt = 0
        self.seen = {}


class Prog:
    NSLOT = 6

    def __init__(self, nc, es):
        self.nc = nc
        self.sems = {}
        self.q = {}
        for nm, eng in (("pe", nc.tensor), ("dve", nc.vector), ("act", nc.scalar), ("pool", nc.gpsimd), ("sp", nc.sync)):
            key = "e_" + nm
            self.sems[key] = es.enter_context(nc.semaphore("sem_" + key))
            self.q[nm] = EngQ(nm, eng, key)
        self.dq = {}
        for nm in ("sp", "act", "pool"):
            keys = []
            for i in range(self.NSLOT):
                key = f"d_{nm}{i}"
                self.sems[key] = es.enter_context(nc.semaphore("sem_" + key))
                keys.append(key)
            self.dq[nm] = [keys, 0]
        self.semval = {k: 0 for k in self.sems}

    def _waits(self, q, E, reads, writes, extra=()):
        need = {}

        def add(k, v):
            if need.get(k, 0) < v:
                need[k] = v
        for b in reads:
            if b.w is not None:
                add(*b.w)
        for b in writes:
            if b.w is not None:
                add(*b.w)
            for k, v in b.r.items():
                add(k, v)
        for k, v in extra:
            add(k, v)
        for k, v in need.items():
            if k == q.key:
                if E == "pe" or v > q.cnt or q.cnt - v >= 6:
                    continue
            if q.seen.get(k, 0) >= v:
                continue
            q.eng.wait_ge(self.sems[k], v)
            q.seen[k] = v

    def op(self, E, fn, reads=(), writes=(), inc=True):
        q = self.q[E]
        self._waits(q, E, reads, writes)
        ins = fn(q.eng)
        if inc:
            q.cnt += 1
            ins.then_inc(self.sems[q.key], 1)
            self.semval[q.key] = q.cnt
            tick = (q.key, q.cnt)
        else:
            tick = (q.key, q.cnt + 1)
        for b in reads:
            if b.r.get(tick[0], 0) < tick[1]:
                b.r[tick[0]] = tick[1]
        for b in writes:
            b.w = tick
            b.r = {}
        return ins

    def dma(self, Q, out, in_, reads=(), writes=(), **kw):
        q = self.q[Q]
        keys, i = self.dq[Q]
        slot = i % self.NSLOT
        rnd = i // self.NSLOT
        key = keys[slot]
        extra = [(key, 16 * rnd)] if rnd > 0 else []
        self._waits(q, Q, reads, writes, extra)
        q.eng.dma_start(out=out, in_=in_, **kw).then_inc(self.sems[key], 16)
        self.dq[Q][1] = i + 1
        tick = (key, 16 * (rnd + 1))
        self.semval[key] = tick[1]
        for b in reads:
            if b.r.get(key, 0) < tick[1]:
                b.r[key] = tick[1]
        for b in writes:
            b.w = tick
            b.r = {}

    def barrier(self):
        for nm, q in self.q.items():
            for k, v in self.semval.items():
                if v == 0 or k == q.key:
                    continue
                if q.seen.get(k, 0) >= v:
                    continue
                q.eng.wait_ge(self.sems[k], v)
                q.seen[k] = v


class Builder:
    def __init__(self, debug=False):
        self.debug = debug
        self.nc = bass.Bass("TRN2", target_bir_lowering=False)
        self.dr = {}

    def din(self, name, shape, dt=F32):
        self.dr[name] = self.nc.dram_tensor(name, list(shape), dt, kind="ExternalInput").ap()
        return self.dr[name]

    def dout(self, name, shape, dt=F32):
        self.dr[name] = self.nc.dram_tensor(name, list(shape), dt, kind="ExternalOutput").ap()
        return self.dr[name]

    def dscr(self, name, shape, dt):
        kind = "ExternalOutput" if (self.debug and name in DEBUG_OUT) else "Internal"
        self.dr[name] = self.nc.dram_tensor(name, list(shape), dt, kind=kind).ap()
        return self.dr[name]


DEBUG_OUT = ("xT", "qT", "kT", "Vs", "U", "mixT", "X3t", "modd")


def build_program(debug=False, nlayers=DEPTH, stop_after=None):
    import contextlib
    B = Builder(debug)
    nc = B.nc
    cst = _constants()
    xs = B.din("xs", [LS, D])
    xp = B.din("xp", [NPS * LP, D])
    ck = B.din("ck", [DEPTH, LP, 1024])
    cv = B.din("cv", [DEPTH, LP, 1024])
    ppd = B.din("pp", [128, NPP])
    w_ada = B.din("w_ada", [DEPTH, D, 6 * D])
    w_in = B.din("w_in", [DEPTH, D, INC])
    w_out = B.din("w_out", [DEPTH, D, D])
    w_gate = B.din("w_gate", [DEPTH, D, FFN])
    w_up = B.din("w_up", [DEPTH, D, FFN])
    w_down = B.din("w_down", [DEPTH, FFN, D])
    hy_w1 = B.din("hy_w1", [DEPTH, 33, 64])
    hy_w2 = B.din("hy_w2", [DEPTH, 64, 64])
    hy_w3 = B.din("hy_w3", [DEPTH, 64, 2048])
    w_pool = B.din("w_pool", [DEPTH, 4, 128, 128])
    cd = {}
    for k, v in cst.items():
        cd[k] = B.din("c_" + k, v.shape, BF16 if v.dtype == ml_dtypes.bfloat16 else F32)
    ys = B.dout("ys", [LS, D])
    yp = B.dout("yp", [NPS * LP, D])
    sk = B.dout("sk", [NPS, DEPTH, LP, 1024])
    sv = B.dout("sv", [NPS, DEPTH, LP, 1024])
    xT = B.dscr("xT", [D, T], F32)
    wb_in = B.dscr("wb_in", [DEPTH, D, INC], BF16)
    wb_out = B.dscr("wb_out", [DEPTH, D, D], BF16)
    wb_gate = B.dscr("wb_gate", [DEPTH, D, FFN], BF16)
    wb_up = B.dscr("wb_up", [DEPTH, D, FFN], BF16)
    wb_down = B.dscr("wb_down", [DEPTH, FFN, D], BF16)
    qT = B.dscr("qT", [H, 128, T], BF16)
    kT = B.dscr("kT", [H, 128, T], BF16)
    Vs = B.dscr("Vs", [T, H, 130], BF16)
    U = B.dscr("U", [D, T], F32)
    mixT = B.dscr("mixT", [D, T], BF16)
    X3t = B.dscr("X3t", [3, T, 512], BF16)
    Kfs = B.dscr("Kfs", [2, 33, 128, 2, 512], BF16)
    Kfp = B.dscr("Kfp", [2, 3, 128, 2, 512], BF16)
    modd = B.dscr("modd", [DEPTH, 128, 192], F32)

    es = contextlib.ExitStack()
    with es:
        P = Prog(nc, es)
        ps = es.enter_context(nc.psum_tensor("ps", [128, 8, 512], F32))
        pbuf = [Buf(f"ps{i}") for i in range(8)]
        pstate = [0]

        held = set()

        def nb():
            while True:
                i = pstate[0] % 8
                pstate[0] += 1
                if i not in held:
                    return ps[:, i, :], pbuf[i]

        def hold(n):
            out = []
            for _ in range(n):
                ap, b = nb()
                held.add(pbuf.index(b))
                out.append((ap, b))
            return out

        def release(banks):
            for ap, b in banks:
                held.discard(pbuf.index(b))

        uniq = [0]

        def sb(name, shape, dt):
            uniq[0] += 1
            t = es2.enter_context(nc.sbuf_tensor(f"{name}_{uniq[0]}", list(shape), dt))
            return t, Buf(name)

        blk = es.enter_context(nc.Block())

        es2 = es
        ppt, ppb = sb("ppt", [128, NPP], F32)
        idf, idfb = sb("idf", [128, 128], F32)
        idb, idbb = sb("idb", [128, 128], BF16)
        oneb, onebb = sb("oneb", [128, 128], BF16)
        boneb, bonebb = sb("boneb", [128, 128], BF16)
        rpf, rpfb = sb("rpf", [128, 128], F32)
        modt, modb = sb("modt", [128, DEPTH, 96, 2], F32)
        gmt, gmb = sb("gmt", [128, DEPTH, 2, 16, 2], F32)
        lamt, lamb = sb("lamt", [128, DEPTH, 2], F32)
        sct, scb = sb("sct", [128, 16, 2], F32)
        P.dma("sp", ppt[:], ppd[:, :], writes=[ppb])
        P.dma("sp", idf[:], cd["ident_f"][:, :], writes=[idfb])
        P.dma("sp", idb[:], cd["ident_b"][:, :], writes=[idbb])
        P.dma("sp", oneb[:], cd["ones_b"][:, :], writes=[onebb])
        P.dma("sp", boneb[:], cd["bones_b"][:, :], writes=[bonebb])
        P.dma("sp", rpf[:], cd["rperm_f"][:, :], writes=[rpfb])
        rpb, rpbb = sb("rpb", [128, 128], BF16)
        P.dma("sp", rpb[:], cd["rperm_b"][:, :], writes=[rpbb])

        def ppa(name, i=0, n=1):
            o, _ = PP[name]
            return ppt[:, o + i:o + i + n]

        def conv_items(l, which):
            specs = {"in": (w_in, wb_in, D, INC), "out": (w_out, wb_out, D, D), "gate": (w_gate, wb_gate, D, FFN),
                     "up": (w_up, wb_up, D, FFN), "down": (w_down, wb_down, FFN, D)}
            items = []
            for nm in which:
                src, dst, rows, cols = specs[nm]
                nrc = rows // 128
                for c0 in range(0, cols, 512):
                    for r0 in range(0, nrc, 8):
                        nr = min(8, nrc - r0)
                        sv_ = src[l].rearrange("(kc p) n -> p kc n", p=128)[:, r0:r0 + nr, c0:c0 + 512]
                        dv_ = dst[l].rearrange("(kc p) n -> p kc n", p=128)[:, r0:r0 + nr, c0:c0 + 512]
                        items.append((sv_, dv_, nr))
            return items

        def phase_W(l, which):
            nonlocal es2
            with contextlib.ExitStack() as es2:
                stg = [sb(f"wstg{i}", [128, 8, 512], F32) for i in range(3)]
                cvt = [sb(f"wcvt{i}", [128, 8, 512], BF16) for i in range(3)]
                i = 0
                specs = {"in": (w_in, wb_in, D, INC), "out": (w_out, wb_out, D, D), "gate": (w_gate, wb_gate, D, FFN),
                         "up": (w_up, wb_up, D, FFN), "down": (w_down, wb_down, FFN, D)}
                for nm in which:
                    src, dst, rows, cols = specs[nm]
                    nrc = rows // 128
                    for c0 in range(0, cols, 512):
                        for r0 in range(0, nrc, 8):
                            nr = min(8, nrc - r0)
                            st_, stb_ = stg[i % 3]
                            cv_, cvb_ = cvt[i % 3]
                            sv_ = src[l].rearrange("(kc p) n -> p kc n", p=128)[:, r0:r0 + nr, c0:c0 + 512]
                            dv_ = dst[l].rearrange("(kc p) n -> p kc n", p=128)[:, r0:r0 + nr, c0:c0 + 512]
                            P.dma("sp", st_[:, 0:nr, :], sv_, writes=[stb_])
                            eng = ("dve", "pool", "act")[i % 3]
                            if eng == "act":
                                P.op("act", lambda e: e.activation(out=cv_[:, 0:nr, :], in_=st_[:, 0:nr, :], func=AF.Copy), reads=[stb_], writes=[cvb_])
                            else:
                                P.op(eng, lambda e: e.tensor_copy(out=cv_[:, 0:nr, :], in_=st_[:, 0:nr, :]), reads=[stb_], writes=[cvb_])
                            P.dma("act", dv_, cv_[:, 0:nr, :], reads=[cvb_])
                            i += 1
                P.barrier()

        def wait_conv(name, l, engs=("sp",)):
            pass

        def phase_mod(l):
            nonlocal es2
            with contextlib.ExitStack() as es2:
                wa = [sb(f"wa{i}", [128, 16, 512], F32) for i in range(2)]
                cvs, cvsb = sb("cvs", [128, 32], F32)
                P.op("act", lambda e: e.activation(out=cvs[:], in_=ppa("cvT", 0, 32), func=AF.Silu), reads=[ppb], writes=[cvsb])
                P.op("dve", lambda e: e.tensor_copy(out=sct[:].rearrange("p a b -> p (a b)"), in_=cvs[:]), reads=[cvsb], writes=[scb])
                for l in [l]:
                    bank, bb = nb()
                    for pc in range(24):
                        wt, wtb = wa[pc % 2]
                        src = w_ada[l].rearrange("(kc p) n -> p kc n", p=128)[:, :, pc * 512:(pc + 1) * 512]
                        P.dma("sp", wt[:], src, writes=[wtb])
                        for j in range(4):
                            m = pc * 4 + j
                            for kc in range(KC):
                                P.op("pe", lambda e: e.matmul(bank[:, 2 * m:2 * m + 2], lhsT=wt[:, kc, j * 128:(j + 1) * 128],
                                                              rhs=sct[:, kc, :], start=(kc == 0), stop=(kc == KC - 1)),
                                     reads=[wtb, scb], writes=[bb], inc=(kc == KC - 1))
                    o, _ = PP[f"bada{l}"]
                    P.op("dve", lambda e: e.tensor_tensor(out=modt[:, l], in0=bank[:, 0:192].rearrange("p (m c) -> p m c", c=2),
                                                          in1=ppt[:, o:o + 96].unsqueeze(2).to_broadcast([128, 96, 2]), op=ALU.add),
                         reads=[bb, ppb], writes=[modb])
                    for wh, nm, sc in ((0, "n1", 1), (1, "n2", 4)):
                        o2, _ = PP[f"{nm}{l}"]
                        P.op("dve", lambda e: e.scalar_tensor_tensor(out=gmt[:, l, wh], in0=modt[:, l, sc * 16:(sc + 1) * 16, :], scalar=1.0,
                                                                     in1=ppt[:, o2:o2 + 16].unsqueeze(2).to_broadcast([128, 16, 2]),
                                                                     op0=ALU.add, op1=ALU.mult),
                             reads=[modb, ppb], writes=[gmb])
                    o3, _ = PP[f"wlam{l}"]
                    tl, tlb = sb(f"tl{l}", [128, 128], F32)
                    sl, slb = sb(f"sl{l}", [128, 2], F32)
                    P.op("dve", lambda e: e.tensor_tensor(out=tl[:].rearrange("p (a b) -> p a b", a=2),
                                                          in0=ppt[:, o3:o3 + 256].rearrange("p (a r b) -> p a r b", a=2, r=2)[:, :, 0, :],
                                                          in1=ppt[:, o3:o3 + 256].rearrange("p (a r b) -> p a r b", a=2, r=2)[:, :, 1, :], op=ALU.mult),
                         reads=[ppb], writes=[tlb])
                    P.op("dve", lambda e: e.tensor_reduce(out=sl[:], in_=tl[:].rearrange("p (a b) -> p a b", a=2), axis=AX.X, op=ALU.add),
                         reads=[tlb], writes=[slb])
                    P.op("act", lambda e: e.activation(out=sl[:], in_=sl[:], func=AF.Exp), reads=[slb], writes=[slb])
                    lam_init = 0.8 - 0.6 * math.exp(-0.3 * l)
                    P.op("dve", lambda e: e.scalar_tensor_tensor(out=lamt[:, l, 0:1], in0=sl[:, 0:1], scalar=lam_init, in1=sl[:, 1:2],
                                                                 op0=ALU.add, op1=ALU.subtract), reads=[slb], writes=[lamb])
                    P.op("dve", lambda e: e.tensor_scalar(out=lamt[:, l, 1:2], in0=lamt[:, l, 0:1], scalar1=-1.0, scalar2=None, op0=ALU.mult),
                         reads=[lamb], writes=[lamb])
                    if debug:
                        P.dma("act", modd[l], modt[:, l].rearrange("p m c -> p (m c)"), reads=[modb])
                P.barrier()

        def phase_xT():
            nonlocal es2
            with contextlib.ExitStack() as es2:
                xin = [sb(f"xin{i}", [128, D], F32) for i in range(2)]
                xst = [sb(f"xst{i}", [128, KC, G], F32) for i in range(2)]
                it = 0
                for g in range(NG):
                    st, stb = xst[g % 2]
                    for tt in range(4):
                        tok = g * G + tt * 128
                        xi, xib = xin[it % 2]
                        it += 1
                        src = xs[tok:tok + 128, :] if tok < LS else xp[tok - LS:tok - LS + 128, :]
                        P.dma("sp", xi[:], src, writes=[xib])
                        for k4 in range(4):
                            bank, bb = nb()
                            for j in range(4):
                                kc = k4 * 4 + j
                                P.op("pe", lambda e: e.transpose(bank[:, j * 128:(j + 1) * 128], xi[:, kc * 128:(kc + 1) * 128], idf[:]),
                                     reads=[xib, idfb], writes=[bb], inc=(j == 3))
                            eng = "act" if k4 % 2 == 0 else "dve"
                            if eng == "act":
                                P.op("act", lambda e: e.activation(out=st[:, k4 * 4:(k4 + 1) * 4, tt * 128:(tt + 1) * 128],
                                                                   in_=bank.rearrange("p (a b) -> p a b", a=4), func=AF.Copy),
                                     reads=[bb], writes=[stb])
                            else:
                                P.op("dve", lambda e: e.tensor_copy(out=st[:, k4 * 4:(k4 + 1) * 4, tt * 128:(tt + 1) * 128],
                                                                    in_=bank.rearrange("p (a b) -> p a b", a=4)),
                                     reads=[bb], writes=[stb])
                    P.dma("act", xT.rearrange("(kc p) t -> p kc t", p=128)[:, :, g * G:(g + 1) * G], st[:], reads=[stb])
                P.barrier()

        def norm_mod(xg, xgb, xn, xnb, hT, hTb, sqb_t, sqbb, rst, rstb, l, wh, cond):
            xgl = xgb if isinstance(xgb, list) else [xgb]
            bank, bb = nb()
            for q4 in range(4):
                rd = [xgl[q4]] if len(xgl) == 4 else xgl
                P.op("act", lambda e: e.activation(out=sqb_t[:, q4 * 4:(q4 + 1) * 4, :], in_=xg[:, q4 * 4:(q4 + 1) * 4, :], func=AF.Square), reads=rd, writes=[sqbb])
                for kc in range(q4 * 4, q4 * 4 + 4):
                    P.op("pe", lambda e: e.matmul(bank, lhsT=oneb[:], rhs=sqb_t[:, kc, :], start=(kc == 0), stop=(kc == KC - 1)),
                         reads=[sqbb, onebb], writes=[bb], inc=(kc % 4 == 3))
            P.op("act", lambda e: e.activation(out=rst[:], in_=bank, func=AF.Ln, bias=EPS, scale=1.0 / D), reads=[bb], writes=[rstb])
            P.op("act", lambda e: e.activation(out=rst[:], in_=rst[:], func=AF.Exp, scale=-0.5), reads=[rstb], writes=[rstb])
            shi = 0 if wh == 0 else 3
            xnq = [Buf(f"xnq{i}") for i in range(4)]
            for q4 in range(4):
                P.op("dve", lambda e: e.tensor_tensor(out=xn[:, q4 * 4:(q4 + 1) * 4, :], in0=xg[:, q4 * 4:(q4 + 1) * 4, :],
                                                      in1=rst[:].unsqueeze(1).to_broadcast([128, 4, G]), op=ALU.mult),
                     reads=xgl + [rstb, sqbb], writes=[xnq[q4], xnb])
                for kc in range(q4 * 4, q4 * 4 + 4):
                    P.op("act", lambda e: e.activation(out=hT[:, kc, :], in_=xn[:, kc, :], func=AF.Identity,
                                                       scale=gmt[:, l, wh, kc, cond:cond + 1], bias=modt[:, l, shi * 16 + kc, cond:cond + 1]),
                         reads=[xnq[q4], xnb, gmb, modb], writes=[hTb])

        def phase_A(l):
            nonlocal es2
            wait_conv("in", l)
            with contextlib.ExitStack() as es2:
                xgs = [sb(f"xg{i}", [128, KC, G], F32) for i in range(1)]
                hT, hTb = sb("hT", [128, KC, G], BF16)
                rst, rstb = sb("rst", [128, G], F32)
                wsl = [sb(f"wsl{i}", [128, KC, 512], BF16) for i in range(3)]
                sqh = [sb(f"sqh{i}", [128, G], BF16) for i in range(2)]
                rsh = [sb(f"rsh{i}", [128, G], F32) for i in range(2)]
                qgs = [sb(f"qgs{i}", [128, G], F32) for i in range(2)]
                qgsb = [sb(f"qgsb{i}", [128, G], BF16) for i in range(2)]
                t1s = [sb(f"t1s{i}", [128, G], F32) for i in range(2)]
                t2s = [sb(f"t2s{i}", [128, G], F32) for i in range(2)]
                qob = [sb(f"qob{i}", [128, G], BF16) for i in range(3)]
                kn32 = [sb(f"kn32{i}", [128, G], F32) for i in range(2)]
                rc, rcb = sb("rc", [128, G], F32)
                rs, rsb = sb("rs", [128, G], F32)
                vst, vstb = sb("vst", [128, 4, H, 130], BF16)
                svst, svstb = sb("svst", [128, 4, 1024], F32)
                skst, skstb = sb("skst", [128, 4, 1024], F32)
                ust = [sb(f"ust{i}", [128, G], F32) for i in range(3)]
                P.op("pool", lambda e: e.memset(vst[:, :, :, 128:130], 1.0), writes=[vstb])
                wi = 0
                ui = 0
                hi = 0
                dq1, dq2, dq2n = [], [], []
                prefetched = [False]

                def dtick():
                    for f_ in dq2:
                        f_()
                    dq2.clear()
                    for f_ in dq1:
                        f_()
                    dq1.clear()
                    dq2.extend(dq2n)
                    dq2n.clear()

                import os as _os
                _gl = [int(x) for x in _os.environ.get("KA_G", ",".join(str(i) for i in range(NG))).split(",")]
                _sl = [int(x) for x in _os.environ.get("KA_S", "0,1,2,3,4,5,6,7,8,9").split(",")]
                for g in _gl:
                    tok0 = g * G
                    samp = g < 8
                    cond = 1 if samp else 0
                    xg, xgb = xgs[0]
                    if not prefetched[0]:
                        P.dma("sp", xg[:], xT.rearrange("(kc p) t -> p kc t", p=128)[:, :, tok0:tok0 + G], writes=[xgb])
                    prefetched[0] = False
                    if samp:
                        P.dma("sp", rc[:], cd["ropec"][:, tok0:tok0 + G], writes=[rcb])
                        P.dma("sp", rs[:], cd["ropes"][:, tok0:tok0 + G], writes=[rsb])
                    norm_mod(xg, xgb, xg, xgb, hT, hTb, hT, hTb, rst, rstb, l, 0, cond)
                    for s in _sl:
                        if s == 3 and len(_gl) == NG and g + 1 < NG:
                            P.dma("sp", xg[:], xT.rearrange("(kc p) t -> p kc t", p=128)[:, :, tok0 + G:tok0 + 2 * G], writes=[xgb])
                            prefetched[0] = True
                        wt, wtb = wsl[wi % 3]
                        wi += 1
                        P.dma("sp", wt[:], wb_in[l].rearrange("(kc p) n -> p kc n", p=128)[:, :, s * 512:(s + 1) * 512], writes=[wtb])
                        if s in (4, 5):
                            for tt in range(4):
                                bank, bb = nb()
                                for kc in range(KC):
                                    P.op("pe", lambda e: e.matmul(bank, lhsT=hT[:, kc, tt * 128:(tt + 1) * 128], rhs=wt[:, kc, :],
                                                                  start=(kc == 0), stop=(kc == KC - 1)),
                                         reads=[hTb, wtb], writes=[bb], inc=(kc == KC - 1))
                                h0 = (s - 4) * 4
                                P.op("act", lambda e: e.activation(out=vst[:, tt, h0:h0 + 4, 0:128], in_=bank.rearrange("p (a b) -> p a b", a=4),
                                                                   func=AF.Copy), reads=[bb], writes=[vstb])
                                dtick()
                                if not samp and not _os.environ.get("KA_NOSVCOPY"):
                                    P.op("act", lambda e: e.activation(out=svst[:, tt, h0 * 128:(h0 + 4) * 128], in_=bank, func=AF.Copy), reads=[bb], writes=[svstb])
                            if s == 5:
                                for tt in range(4):
                                    tk = tok0 + tt * 128
                                    P.dma("act", Vs[tk:tk + 128].rearrange("t h e -> t (h e)"), vst[:, tt].rearrange("p h e -> p (h e)"), reads=[vstb])
                                    if not samp and not _os.environ.get("KA_NOSVDMA"):
                                        sq_ = (tk - LS) // LP
                                        t_ = (tk - LS) % LP
                                        P.dma("act", sv[sq_, l, t_:t_ + 128, :], svst[:, tt, :], reads=[svstb])
                            continue
                        for j in range(4):
                            cc = s * 4 + j
                            bank, bb = nb()
                            for kc in range(KC):
                                P.op("pe", lambda e: e.matmul(bank, lhsT=wt[:, kc, j * 128:(j + 1) * 128], rhs=hT[:, kc, :],
                                                              start=(kc == 0), stop=(kc == KC - 1)),
                                     reads=[hTb, wtb], writes=[bb], inc=(kc == KC - 1))
                            if s >= 6:
                                ut, utb = ust[ui % 3]
                                ui += 1
                                P.op("act", lambda e: e.activation(out=ut[:], in_=bank, func=AF.Copy), reads=[bb], writes=[utb])
                                ch = (cc - 24) * 128
                                P.dma("act", U[ch:ch + 128, tok0:tok0 + G], ut[:], reads=[utb])
                                dtick()
                                continue
                            isk = s >= 2
                            head = cc % 8
                            sq_t, sq_b = sqh[hi % 2]
                            rh, rhb = rsh[hi % 2]
                            qg_t, qg_b = (qgsb if samp else qgs)[hi % 2]
                            t1, t1b = t1s[hi % 2]
                            t2, t2b = t2s[hi % 2]
                            qo, qo_b = qob[hi % 3]
                            k32, k32b = kn32[hi % 2]
                            hi += 1
                            gname = f"kg{l}" if isk else f"qg{l}"
                            P.op("act", lambda e: e.activation(out=sq_t[:], in_=bank, func=AF.Square), reads=[bb], writes=[sq_b])
                            P.op("act", lambda e: e.activation(out=qg_t[:], in_=bank, func=AF.Identity, scale=ppa(gname)), reads=[bb, ppb], writes=[qg_b])

                            def post1(sq_t=sq_t, sq_b=sq_b, rh=rh, rhb=rhb, qg_t=qg_t, qg_b=qg_b, t1=t1, t1b=t1b, t2=t2, t2b=t2b,
                                      qo=qo, qo_b=qo_b, k32=k32, k32b=k32b, head=head, isk=isk, samp=samp, tok0=tok0):
                                bk2, bb2 = nb()
                                P.op("pe", lambda e: e.matmul(bk2, lhsT=boneb[:], rhs=sq_t[:], start=True, stop=True), reads=[sq_b, bonebb], writes=[bb2])
                                P.op("act", lambda e: e.activation(out=rh[:], in_=bk2, func=AF.Ln, bias=EPS, scale=1.0 / 64), reads=[bb2], writes=[rhb])
                                P.op("act", lambda e: e.activation(out=rh[:], in_=rh[:], func=AF.Exp, scale=-0.5), reads=[rhb], writes=[rhb])
                                if samp:
                                    bk3, bb3 = nb()
                                    P.op("pe", lambda e: e.matmul(bk3, lhsT=rpb[:], rhs=qg_t[:], start=True, stop=True), reads=[qg_b, rpbb], writes=[bb3])
                                    P.op("dve", lambda e: e.tensor_tensor(out=t1[:], in0=qg_t[:], in1=rc[:], op=ALU.mult), reads=[qg_b, rcb], writes=[t1b])
                                    P.op("dve", lambda e: e.tensor_tensor(out=t2[:], in0=bk3, in1=rs[:], op=ALU.mult), reads=[bb3, rsb], writes=[t2b])
                                    P.op("dve", lambda e: e.tensor_tensor(out=t1[:], in0=t1[:], in1=t2[:], op=ALU.add), reads=[t1b, t2b], writes=[t1b])
                                    P.op("dve", lambda e: e.tensor_tensor(out=qo[:], in0=t1[:], in1=rh[:], op=ALU.mult), reads=[t1b, rhb], writes=[qo_b])
                                else:
                                    P.op("dve", lambda e: e.tensor_tensor(out=qo[:], in0=qg_t[:], in1=rh[:], op=ALU.mult), reads=[qg_b, rhb], writes=[qo_b])
                                    if isk:
                                        P.op("dve", lambda e: e.tensor_tensor(out=k32[:], in0=qg_t[:], in1=rh[:], op=ALU.mult), reads=[qg_b, rhb], writes=[k32b])

                                        def post2(k32=k32, k32b=k32b, head=head):
                                            bk4, bb4 = nb()
                                            for tt in range(4):
                                                P.op("pe", lambda e: e.transpose(bk4[:, tt * 128:(tt + 1) * 128], k32[:, tt * 128:(tt + 1) * 128], idf[:]),
                                                     reads=[k32b, idfb], writes=[bb4], inc=(tt == 3))
                                            P.op("act", lambda e: e.activation(out=skst[:, :, head * 128:(head + 1) * 128],
                                                                               in_=bk4.rearrange("p (a b) -> p a b", a=4), func=AF.Copy), reads=[bb4], writes=[skstb])
                                        dq2n.append(post2)
                                dst = (kT if isk else qT)[head, :, tok0:tok0 + G]
                                P.dma("act", dst, qo[:], reads=[qo_b])
                            dtick()
                            dq1.append(post1)
                        if s == 3 and not samp:
                            dtick()
                            dtick()
                            for tt in range(4):
                                tk = tok0 + tt * 128
                                sq_ = (tk - LS) // LP
                                t_ = (tk - LS) % LP
                                P.dma("act", sk[sq_, l, t_:t_ + 128, :], skst[:, tt, :], reads=[skstb])
                    dtick()
                    dtick()
                P.barrier()

        def phase_B(l):
            nonlocal es2
            lam_init = 0.8 - 0.6 * math.exp(-0.3 * l)
            with contextlib.ExitStack() as es2:
                ckl, cklb = sb("ckl", [128, 2, 1024], F32)
                ckT, ckTb = sb("ckT", [128, H, LP], BF16)
                cvb, cvbb = sb("cvb", [128, 2, H, 130], BF16)
                sln, slnb = sb("sln", [128, 128], F32)
                kts = [sb(f"kts{i}", [128, LS + LP], BF16) for i in range(2)]
                qts = [sb(f"qts{i}", [128, 2, LS], BF16) for i in range(2)]
                vts = [sb(f"vts{i}", [128, 34, 130], BF16) for i in range(2)]
                pts = [sb(f"pts{i}", [128, 2, 512], BF16) for i in range(4)]
                ats, atsb = sb("ats", [128, 512], BF16)
                o1s = [sb(f"o1s{i}", [128, 128], F32) for i in range(4)]
                ars = [sb(f"ars{i}", [128, 128], F32) for i in range(4)]
                abs_ = [sb(f"abs{i}", [128, 128], BF16) for i in range(4)]
                rr = [sb(f"rr{i}", [128, 4], F32) for i in range(4)]
                P.dma("sp", ckl[:], ck[l].rearrange("(a p) n -> p a n", p=128), writes=[cklb])
                for tc in range(2):
                    for h in range(H):
                        bank, bb = nb()
                        P.op("pe", lambda e: e.transpose(bank[:, 0:128], ckl[:, tc, h * 128:(h + 1) * 128], idf[:]), reads=[cklb, idfb], writes=[bb])
                        P.op("act", lambda e: e.activation(out=ckT[:, h, tc * 128:(tc + 1) * 128], in_=bank[:, 0:128], func=AF.Copy), reads=[bb], writes=[ckTb])
                ckl2, ckl2b = sb("ckl2", [128, 2, 1024], F32)
                P.dma("sp", ckl2[:], cv[l].rearrange("(a p) n -> p a n", p=128), writes=[ckl2b])
                P.op("pool", lambda e: e.memset(cvb[:, :, :, 128:130], 1.0), writes=[cvbb])
                P.op("dve", lambda e: e.tensor_copy(out=cvb[:, :, :, 0:128], in_=ckl2[:].rearrange("p a (h e) -> p a h e", h=H)), reads=[ckl2b], writes=[cvbb])
                o, _ = PP[f"subln{l}"]
                P.op("dve", lambda e: e.tensor_scalar(out=sln[:], in0=ppt[:, o:o + 128], scalar1=(1.0 - lam_init), scalar2=None, op0=ALU.mult),
                     reads=[ppb], writes=[slnb])
                seqs = [(0, LS, True)] + [(LS + s * LP, LP, False) for s in range(NPS)]
                it = 0
                pi = 0
                ei = 0
                qbi = 0
                sci = 0
                bstg = [sb(f"bstg{i}", [128, 8, 512], F32) for i in range(2)]
                bcvt = [sb(f"bcvt{i}", [128, 8, 512], BF16) for i in range(2)]
                bgq = conv_items(l, ["out", "gate", "up", "down"]) + (conv_items(l + 1, ["in"]) if l + 1 < nlayers else [])
                bgstate = [0, None]

                def bg_step():
                    if bgstate[1] is not None:
                        dv_, cv_, cvb_, nr = bgstate[1]
                        P.dma("sp", dv_, cv_[:, 0:nr, :], reads=[cvb_])
                        bgstate[1] = None
                    if bgq:
                        sv_, dv_, nr = bgq.pop(0)
                        i_ = bgstate[0]
                        bgstate[0] += 1
                        st_, stb_ = bstg[i_ % 2]
                        cv_, cvb_ = bcvt[i_ % 2]
                        P.dma("sp", st_[:, 0:nr, :], sv_, writes=[stb_])
                        P.op("pool", lambda e: e.tensor_copy(out=cv_[:, 0:nr, :], in_=st_[:, 0:nr, :]), reads=[stb_], writes=[cvb_])
                        bgstate[1] = (dv_, cv_, cvb_, nr)
                held.update(range(7))
                for (qz_, qzb_) in qts:
                    P.op("pool", lambda e: e.memset(qz_[:], 0.0), writes=[qzb_])
                QB = 256
                nqt = 2
                pend = []
                atsl = [sb(f"atsd{i}", [128, 256], BF16) for i in range(2)]
                epc = [0]

                def make_epi(regs, h, tok0, q0):
                    at_, atb_ = atsl[epc[0] % 2]
                    epc[0] += 1
                    bufsets = []
                    for qi in range(nqt):
                        k_ = (epc[0] * 2 + qi) % 4
                        bufsets.append(k_)

                    def st1():
                        for qi in range(nqt):
                            (r1, rb1, _), (r2, rb2, _) = regs[qi * 2], regs[qi * 2 + 1]
                            k_ = bufsets[qi]
                            o1, o1b = o1s[k_]
                            ar, arb = ars[k_]
                            rt, rtb = rr[k_]
                            P.op("dve", lambda e: e.reciprocal(out=rt[:, 0:1], in_=r1[:, 128:129]), reads=[rb1], writes=[rtb])
                            P.op("dve", lambda e: e.reciprocal(out=rt[:, 1:2], in_=r2[:, 128:129]), reads=[rb2], writes=[rtb])
                            P.op("dve", lambda e: e.tensor_tensor(out=rt[:, 1:2], in0=rt[:, 1:2], in1=lamt[:, l, 1:2], op=ALU.mult), reads=[rtb, lamb], writes=[rtb])
                            P.op("dve", lambda e: e.tensor_scalar(out=o1[:], in0=r1[:, 0:128], scalar1=rt[:, 0:1], scalar2=None, op0=ALU.mult), reads=[rb1, rtb], writes=[o1b])
                            P.op("dve", lambda e: e.scalar_tensor_tensor(out=ar[:], in0=r2[:, 0:128], scalar=rt[:, 1:2], in1=o1[:], op0=ALU.mult, op1=ALU.add),
                                 reads=[rb2, rtb, o1b], writes=[arb])

                    def st2():
                        for qi in range(nqt):
                            k_ = bufsets[qi]
                            o1, o1b = o1s[k_]
                            ar, arb = ars[k_]
                            rt, rtb = rr[k_]
                            P.op("act", lambda e: e.activation(out=o1[:], in_=ar[:], func=AF.Square, accum_out=rt[:, 2:3]), reads=[arb], writes=[o1b, rtb])
                            P.op("act", lambda e: e.activation(out=rt[:, 3:4], in_=rt[:, 2:3], func=AF.Ln, bias=EPS, scale=1.0 / 128), reads=[rtb], writes=[rtb])
                            P.op("act", lambda e: e.activation(out=rt[:, 3:4], in_=rt[:, 3:4], func=AF.Exp, scale=-0.5), reads=[rtb], writes=[rtb])

                    def st3():
                        for qi in range(nqt):
                            k_ = bufsets[qi]
                            ar, arb = ars[k_]
                            ab, abb = abs_[k_]
                            rt, rtb = rr[k_]
                            P.op("dve", lambda e: e.scalar_tensor_tensor(out=ab[:], in0=ar[:], scalar=rt[:, 3:4], in1=sln[:], op0=ALU.mult, op1=ALU.mult),
                                 reads=[arb, rtb, slnb], writes=[abb])
                            bk5, bb5 = nb()
                            bk5b = bk5.bitcast(BF16)
                            P.op("pe", lambda e: e.transpose(bk5b[:, 0:128], ab[:], idb[:]), reads=[abb, idbb], writes=[bb5])
                            P.op("dve", lambda e: e.tensor_copy(out=at_[:, qi * 128:(qi + 1) * 128], in_=bk5b[:, 0:128]), reads=[bb5], writes=[atb_])
                        P.dma("sp", mixT[h * 128:(h + 1) * 128, tok0 + q0:tok0 + q0 + QB], at_[:, 0:QB], reads=[atb_])
                    return [st1, st2, st3]

                for (tok0, L, cache) in seqs:
                    nkc = L // 128 + (2 if cache else 0)
                    for h in range(H):
                        kt, ktb = kts[it % 2]
                        qt_, qtb = qts[it % 2]
                        vt, vtb = vts[it % 2]
                        it += 1
                        P.dma("sp", kt[:, 0:L], kT[h, :, tok0:tok0 + L], writes=[ktb])
                        P.dma("sp", qt_[0:64, 0, 0:L], qT[h, 0:64, tok0:tok0 + L], writes=[qtb])
                        P.dma("sp", qt_[64:128, 1, 0:L], qT[h, 64:128, tok0:tok0 + L], writes=[qtb])
                        P.dma("sp", vt[:, 0:L // 128, :], Vs[tok0:tok0 + L, h, :].rearrange("(a p) e -> p a e", p=128), writes=[vtb])
                        if cache:
                            P.op("pool", lambda e: e.tensor_copy(out=kt[:, L:L + LP], in_=ckT[:, h, :]), reads=[ckTb], writes=[ktb])
                            P.op("pool", lambda e: e.tensor_copy(out=vt[:, L // 128:L // 128 + 2, :], in_=cvb[:, :, h, :]), reads=[cvbb], writes=[vtb])
                        for qb in range(L // QB):
                            q0 = qb * QB
                            aset = qbi % 2
                            qbi += 1
                            bg_step()
                            while len(pend) > 1:
                                for stg_ in pend.pop(0):
                                    stg_()
                            bA, bB = 2 * aset, 2 * aset + 1
                            regs = [(ps[:, bA, 0:130], pbuf[bA], True), (ps[:, bA, 130:260], pbuf[bA], False),
                                    (ps[:, bA, 260:390], pbuf[bA], False), (ps[:, bB, 0:130], pbuf[bB], True)]
                            sbanks = {}

                            def emit_S(kc):
                                nonlocal sci
                                bi_ = 4 + (sci % 3)
                                sci += 1
                                sbanks[kc] = (ps[:, bi_, :], pbuf[bi_])
                                P.op("pe", lambda e: e.matmul(ps[:, bi_, :].rearrange("p (a b) -> p a b", a=2), lhsT=kt[:, kc * 128:(kc + 1) * 128],
                                                              rhs=qt_[:, :, q0:q0 + QB], start=True, stop=True),
                                     reads=[ktb, qtb], writes=[pbuf[bi_]], inc=True)
                            emit_S(0)
                            if nkc > 1:
                                emit_S(1)
                            for kc in range(nkc):
                                if pend and kc in (2, 6, 10):
                                    pend[0].pop(0)()
                                    if not pend[0]:
                                        pend.pop(0)
                                if kc + 2 < nkc:
                                    emit_S(kc + 2)
                                bk, bkb = sbanks.pop(kc)
                                pt, ptb = pts[pi % 4]
                                pi += 1
                                P.op("act", lambda e: e.activation(out=pt[:].rearrange("p a b -> p (a b)")[:, 0:512], in_=bk, func=AF.Exp, scale=0.125), reads=[bkb], writes=[ptb])
                                ptf = pt[:].rearrange("p a b -> p (a b)")
                                for qi in range(nqt):
                                    for j in range(2):
                                        ra, rb_, first = regs[qi * 2 + j]
                                        c0 = j * 256 + qi * 128
                                        P.op("pe", lambda e: e.matmul(ra, lhsT=ptf[:, c0:c0 + 128], rhs=vt[:, kc, :],
                                                                      start=(kc == 0 and first), stop=(kc == nkc - 1), skip_group_check=True),
                                             reads=[ptb, vtb], writes=[rb_], inc=(kc == nkc - 1))
                            pend.append(make_epi(regs, h, tok0, q0))
                while pend:
                    for stg_ in pend.pop(0):
                        stg_()
                while bgq or bgstate[1] is not None:
                    bg_step()
                held.clear()
                P.barrier()

        def phase_C1(l):
            nonlocal es2
            with contextlib.ExitStack() as es2:
                ubuf = [sb(f"ub{i}", [128, 8 + T + 8], F32) for i in range(2)]
                cvo, cvob = sb("cvo", [128, T], F32)
                tst = [sb(f"tst{i}", [128, 4, 128], BF16) for i in range(3)]
                seqs = [(0, LS)] + [(LS + s * LP, LP) for s in range(NPS)]
                ti = 0
                for ch in range(12):
                    ub, ubb = ubuf[ch % 2]
                    P.op("pool", lambda e: e.memset(ub[:], 0.0), writes=[ubb])
                    offs = []
                    pos = 1
                    for (tok0, L) in seqs:
                        offs.append(pos)
                        P.dma("sp", ub[:, pos:pos + L], U[ch * 128:(ch + 1) * 128, tok0:tok0 + L], writes=[ubb])
                        pos += L + 1
                    o, _ = PP[f"wsc{l}"]
                    ob, _ = PP[f"bsc{l}"]
                    for si, (tok0, L) in enumerate(seqs):
                        p0 = offs[si]
                        eng = "dve"
                        P.op("act", lambda e: e.activation(out=cvo[:, tok0:tok0 + L], in_=ub[:, p0:p0 + L], func=AF.Identity,
                                                           scale=ppt[:, o + 12 + ch:o + 12 + ch + 1], bias=ppt[:, ob + ch:ob + ch + 1]),
                             reads=[ubb, ppb], writes=[cvob])
                        P.op("dve", lambda e: e.scalar_tensor_tensor(out=cvo[:, tok0:tok0 + L], in0=ub[:, p0 - 1:p0 - 1 + L], scalar=ppt[:, o + ch:o + ch + 1],
                                                                     in1=cvo[:, tok0:tok0 + L], op0=ALU.mult, op1=ALU.add), reads=[ubb, ppb, cvob], writes=[cvob])
                        P.op("dve", lambda e: e.scalar_tensor_tensor(out=cvo[:, tok0:tok0 + L], in0=ub[:, p0 + 1:p0 + 1 + L], scalar=ppt[:, o + 24 + ch:o + 24 + ch + 1],
                                                                     in1=cvo[:, tok0:tok0 + L], op0=ALU.mult, op1=ALU.add), reads=[ubb, ppb, cvob], writes=[cvob])
                    part, cc = divmod(ch, 4)
                    for t4 in range(T // 512):
                        bank, bb = nb()
                        for j in range(4):
                            tk = t4 * 512 + j * 128
                            P.op("pe", lambda e: e.transpose(bank[:, j * 128:(j + 1) * 128], cvo[:, tk:tk + 128], idf[:]), reads=[cvob, idfb], writes=[bb], inc=(j == 3))
                        ts_, tsb = tst[ti % 3]
                        ti += 1
                        if ti % 2 == 0:
                            P.op("act", lambda e: e.activation(out=ts_[:], in_=bank.rearrange("p (a b) -> p a b", a=4), func=AF.Copy), reads=[bb], writes=[tsb])
                        else:
                            P.op("dve", lambda e: e.tensor_copy(out=ts_[:], in_=bank.rearrange("p (a b) -> p a b", a=4)), reads=[bb], writes=[tsb])
                        P.dma("act", X3t[part, t4 * 512:(t4 + 1) * 512, cc * 128:(cc + 1) * 128].rearrange("(a p) c -> p a c", p=128), ts_[:], reads=[tsb])
                P.barrier()
            with contextlib.ExitStack() as es2:
                ubuf = [sb(f"pb{i}", [128, NPS + 1, 8 + LS + 8], F32) for i in range(1)]
                a1, a1b = sb("pa1", [128, 8 + LS + 8], F32)
                a2, a2b = sb("pa2", [128, 8 + LS + 8], F32)
                pl, plb = sb("ppl", [128, LS], F32)
                plb16 = [sb(f"plb{i}", [128, LS], BF16) for i in range(1)]
                rcs_t, rcsb = sb("rcs", [128, LS], F32)
                wp, wpb = sb("wp", [128, 128], F32)
                wpbf, wpbfb = sb("wpbf", [128, 128], BF16)
                pos_, posb = sb("pos", [128, 512], BF16)
                seqs = [(0, LS, "rcs")] + [(LS + s * LP, LP, "rcp") for s in range(NPS)]
                for g, w in enumerate((2, 4, 8, 16)):
                    ub, ubb = ubuf[0]
                    P.op("pool", lambda e: e.memset(ub[:], 0.0), writes=[ubb])
                    P.dma("sp", wp[:], w_pool[l, g], writes=[wpb])
                    P.op("dve", lambda e: e.tensor_copy(out=wpbf[:], in_=wp[:]), reads=[wpb], writes=[wpbfb])
                    for si, (tok0, L, rcn) in enumerate(seqs):
                        P.dma("sp", ub[:, si, 8:8 + L], U[1536 + g * 128:1536 + (g + 1) * 128, tok0:tok0 + L], writes=[ubb])
                    for si, (tok0, L, rcn) in enumerate(seqs):
                        W = 8 + L + 8
                        P.dma("sp", rcs_t[:, 0:L], cd[rcn][:, g, :], writes=[rcsb])
                        src = ub[:, si, :]
                        cur, curb = None, None
                        lv = 1
                        bufs = [(a1, a1b), (a2, a2b)]
                        bi = 0
                        prev, prevb = src, ubb
                        while lv < w:
                            dst_, dstb = bufs[bi % 2]
                            bi += 1
                            n = W - lv
                            P.op("pool", lambda e: e.tensor_tensor(out=dst_[:, 0:n], in0=prev[:, 0:n], in1=prev[:, lv:lv + n], op=ALU.add),
                                 reads=[prevb], writes=[dstb])
                            prev, prevb = dst_, dstb
                            lv *= 2
                        s0 = 8 - w // 2
                        P.op("dve", lambda e: e.tensor_tensor(out=pl[:, 0:L], in0=prev[:, s0:s0 + L], in1=rcs_t[:, 0:L], op=ALU.mult), reads=[prevb, rcsb], writes=[plb])
                        pb16, pb16b = plb16[0]
                        P.op("dve", lambda e: e.tensor_tensor(out=pb16[:, 0:L], in0=pl[:, 0:L], in1=ub[:, si, 8:8 + L], op=ALU.subtract), reads=[plb, ubb], writes=[pb16b])
                        o, _ = PP[f"psc{l}"]
                        for c0 in range(0, L, 512):
                            n = min(512, L - c0)
                            bank, bb = nb()
                            P.op("pe", lambda e: e.matmul(bank[:, 0:n], lhsT=wpbf[:], rhs=pb16[:, c0:c0 + n], start=True, stop=True), reads=[wpbfb, pb16b], writes=[bb])
                            P.op("act", lambda e: e.activation(out=pos_[:, 0:n], in_=bank[:, 0:n], func=AF.Identity, scale=ppt[:, o + g:o + g + 1]), reads=[bb, ppb], writes=[posb])
                            P.dma("act", mixT[1536 + g * 128:1536 + (g + 1) * 128, tok0 + c0:tok0 + c0 + n], pos_[:, 0:n], reads=[posb])
                P.barrier()

        def phase_C2(l, L, nseq, tokbase, tck, tsk, gnk, zfk, ntlk, Kf):
            nonlocal es2
            NT = L // 128
            NF = (L + 1 + 127) // 128
            NB = nseq * 512
            Tc = cd[tck]
            Ts = cd[tsk]
            with contextlib.ExitStack() as es_outer:
                es2 = es_outer
                h2, h2b = sb("h2", [64, L], BF16)
                w3, w3b = sb("w3", [64, 2048], F32)
                w3h, w3hb = sb("w3h", [64, 2048], BF16)
                fp_, fpb = sb("fp", [64, 8], F32)
                P.dma("sp", w3[:], hy_w3[l], writes=[w3b])
                P.op("dve", lambda e: e.tensor_copy(out=w3h[:], in_=w3[:]), reads=[w3b], writes=[w3hb])
                for li, (fn_, bn_) in enumerate(((f"hf1{l}", f"hb1{l}"), (f"hf2{l}", f"hb2{l}"))):
                    of, _ = PP[fn_]
                    obb, _ = PP[bn_]
                    c4 = li * 4
                    P.op("dve", lambda e: e.tensor_scalar(out=fp_[:, c4:c4 + 1], in0=ppt[0:64, of:of + 1], scalar1=0.5, scalar2=None, op0=ALU.mult), reads=[ppb], writes=[fpb])
                    P.op("dve", lambda e: e.tensor_tensor(out=fp_[:, c4 + 1:c4 + 2], in0=fp_[:, c4:c4 + 1], in1=ppt[0:64, obb:obb + 1], op=ALU.mult), reads=[ppb, fpb], writes=[fpb])
                    P.op("dve", lambda e: e.tensor_scalar(out=fp_[:, c4 + 2:c4 + 4], in0=fp_[:, c4:c4 + 2], scalar1=0.5, scalar2=None, op0=ALU.mult), reads=[fpb], writes=[fpb])
                with contextlib.ExitStack() as es_inner:
                    es2 = es_inner
                    zf, zfb = sb("zf", [33, L], F32)
                    w1, w1b = sb("w1", [33, 64], F32)
                    w2, w2b = sb("w2", [64, 64], F32)
                    h1, h1b = sb("h1", [64, L], F32)
                    s2, s2b = sb("s2", [64, 512], F32)
                    s4, s4b = sb("s4", [64, 512], F32)
                    P.dma("sp", zf[:], cd[zfk][:, :], writes=[zfb])
                    P.dma("sp", w1[:], hy_w1[l], writes=[w1b])
                    P.dma("sp", w2[:], hy_w2[l], writes=[w2b])

                    def sin_layer(li, wt, wtb_, K, src, srcb, dst, dstb):
                        c4 = li * 4
                        for c0 in range(0, L, 512):
                            n = min(512, L - c0)
                            bank, bb = nb()
                            P.op("pe", lambda e: e.matmul(bank[0:64, 0:n], lhsT=wt[0:K, :], rhs=src[0:K, c0:c0 + n], start=True, stop=True), reads=[wtb_, srcb], writes=[bb])
                            P.op("act", lambda e: e.activation(out=s2[:, 0:n], in_=bank[0:64, 0:n], func=AF.Sin, scale=fp_[:, c4:c4 + 1], bias=fp_[:, c4 + 1:c4 + 2]), reads=[bb, fpb], writes=[s2b])
                            P.op("act", lambda e: e.activation(out=s4[:, 0:n], in_=bank[0:64, 0:n], func=AF.Sin, scale=fp_[:, c4 + 2:c4 + 3], bias=fp_[:, c4 + 3:c4 + 4]), reads=[bb, fpb], writes=[s4b])
                            P.op("dve", lambda e: e.tensor_tensor(out=s4[:, 0:n], in0=s4[:, 0:n], in1=s4[:, 0:n], op=ALU.mult), reads=[s4b], writes=[s4b])
                            P.op("dve", lambda e: e.tensor_scalar(out=s4[:, 0:n], in0=s4[:, 0:n], scalar1=-4.0, scalar2=2.0, op0=ALU.mult, op1=ALU.add), reads=[s4b], writes=[s4b])
                            P.op("dve", lambda e: e.tensor_tensor(out=dst[:, c0:c0 + n], in0=s4[:, 0:n], in1=s2[:, 0:n], op=ALU.mult), reads=[s4b, s2b], writes=[dstb])

                    sin_layer(0, w1, w1b, 33, zf, zfb, h1, h1b)
                    sin_layer(1, w2, w2b, 64, h1, h1b, h2, h2b)
                    P.barrier()
                es2 = es_outer
                gn, gnb = sb("gn", [128, NF], F32)
                ntl, ntlb = sb("ntl", [128, NT], F32)
                dlt, dltb = sb("dlt", [128, 512], F32)
                ksd, ksdb = sb("ksd", [128, NT, 2, 512], BF16)
                win, winb = sb("win", [128, 512], F32)
                wf = [sb(f"wf{i}", [128, 512], F32) for i in range(2)]
                wab = [sb(f"wab{i}", [128, 512], BF16) for i in range(2)]
                rn, rnb = sb("rn", [128, 512], F32)
                tsl = [sb(f"tsl{i}", [128, 2, NT * 128], BF16) for i in range(2)]
                kfo = [sb(f"kfo{i}", [128, 2, 512], BF16) for i in range(2)]
                P.dma("sp", gn[:], cd[gnk][:, :], writes=[gnb])
                P.dma("sp", ntl[:], cd[ntlk][:, :], writes=[ntlb])
                P.dma("sp", dlt[:], cd["delta"][:, :], writes=[dltb])
                ki = 0
                for o_ in range(2):
                    nacc = hold(1)
                    for tc in range(NT):
                        P.op("act", lambda e: e.activation(out=win[:], in_=dlt[:], func=AF.Exp, scale=ntl[:, tc:tc + 1]), reads=[dltb, ntlb], writes=[winb])
                        for dr in range(2):
                            bank, bb = nb()
                            cb = dr * 1024 + o_ * 512
                            P.op("pe", lambda e: e.matmul(bank, lhsT=h2[:, tc * 128:(tc + 1) * 128], rhs=w3h[:, cb:cb + 512], start=True, stop=True), reads=[h2b, w3hb], writes=[bb])
                            wt_, wtb_ = wf[dr]
                            P.op("dve", lambda e: e.scalar_tensor_tensor(out=wt_[:], in0=win[:], scalar=0.05, in1=bank, op0=ALU.add, op1=ALU.mult), reads=[winb, bb], writes=[wtb_])
                            if dr == 1 and tc == 0:
                                P.op("dve", lambda e: e.memset(wt_[0:1, :], 0.0), writes=[wtb_])
                            ab_, abb_ = wab[dr]
                            P.op("act", lambda e: e.activation(out=ab_[:], in_=wt_[:], func=AF.Abs), reads=[wtb_], writes=[abb_])
                            P.op("pe", lambda e: e.matmul(nacc[0][0], lhsT=oneb[:], rhs=ab_[:], start=(tc == 0 and dr == 0), stop=(tc == NT - 1 and dr == 1)),
                                 reads=[abb_, onebb], writes=[nacc[0][1]], inc=True)
                        P.op("dve", lambda e: e.tensor_tensor(out=ksd[:, tc, 0, :], in0=wf[0][0][:], in1=wf[1][0][:], op=ALU.add), reads=[wf[0][1], wf[1][1]], writes=[ksdb])
                        P.op("pool", lambda e: e.tensor_tensor(out=ksd[:, tc, 1, :], in0=wf[0][0][:], in1=wf[1][0][:], op=ALU.subtract), reads=[wf[0][1], wf[1][1]], writes=[ksdb])
                    P.op("dve", lambda e: e.reciprocal(out=rn[:], in_=nacc[0][0]), reads=[nacc[0][1]], writes=[rnb])
                    release(nacc)
                    for fc in range(NF):
                        tl_, tlb_ = tsl[fc % 2]
                        P.dma("sp", tl_[:, 0, :], Tc[fc, :, 0:NT, :].rearrange("p a b -> p (a b)"), writes=[tlb_])
                        P.dma("sp", tl_[:, 1, :], Ts[fc, :, 0:NT, :].rearrange("p a b -> p (a b)"), writes=[tlb_])
                        ko, kob = kfo[ki % 2]
                        ki += 1
                        for cs in range(2):
                            bank, bb = nb()
                            for tc in range(NT):
                                P.op("pe", lambda e: e.matmul(bank, lhsT=tl_[:, cs, tc * 128:(tc + 1) * 128], rhs=ksd[:, tc, cs, :], start=(tc == 0), stop=(tc == NT - 1)),
                                     reads=[tlb_, ksdb], writes=[bb], inc=(tc == NT - 1))
                            P.op("dve", lambda e: e.scalar_tensor_tensor(out=ko[:, cs, :], in0=bank, scalar=gn[:, fc:fc + 1], in1=rn[:], op0=ALU.mult, op1=ALU.mult),
                                 reads=[bb, gnb, rnb], writes=[kob])
                        P.dma("act", Kf[o_, fc], ko[:], reads=[kob])
                P.barrier()
            with contextlib.ExitStack() as es2:
                zt, ztb = sb("zt", [128, NT, NB], BF16)
                Y, Yb = sb("Y", [128, NF, 2, NB], BF16)
                tsl = [sb(f"tsd{i}", [128, 2, NF * 128], BF16) for i in range(2)]
                kfl = [sb(f"kfl{i}", [128, 2, 512], BF16) for i in range(2)]
                zr, zrb = sb("zr", [128, 2, 512], F32)
                ta, tab = sb("ta", [128, 512], F32)
                tb_, tbb = sb("tb", [128, 512], F32)
                xg_ = [sb(f"xgt{i}", [128, 512], BF16) for i in range(2)]
                ga, gab = sb("ga", [128, 512], F32)
                hyo = [sb(f"hyo{i}", [128, 512], BF16) for i in range(2)]
                hts = [sb(f"hts{i}", [128, 4, 128], BF16) for i in range(2)]
                for s_ in range(nseq):
                    tk0 = tokbase + s_ * L
                    P.dma("sp", zt[:, :, s_ * 512:(s_ + 1) * 512], X3t[0, tk0:tk0 + L, :].rearrange("(a p) c -> p a c", p=128), writes=[ztb])
                od, _ = PP[f"hydrep{l}"]
                si_ = 0
                for o_ in range(2):
                    for fc in range(NF):
                        tl_, tlb_ = tsl[si_ % 2]
                        si_ += 1
                        P.dma("sp", tl_[:, 0, 0:NT * 128], Tc[fc, :, 0:NT, :].rearrange("p a b -> p (a b)"), writes=[tlb_])
                        P.dma("sp", tl_[:, 1, 0:NT * 128], Ts[fc, :, 0:NT, :].rearrange("p a b -> p (a b)"), writes=[tlb_])
                        kf_, kfb_ = kfl[fc % 2]
                        P.dma("sp", kf_[:], Kf[o_, fc], writes=[kfb_])
                        for s_ in range(nseq):
                            bks = []
                            for cs in range(2):
                                bank, bb = nb()
                                bks.append((bank, bb))
                                for tc in range(NT):
                                    P.op("pe", lambda e: e.matmul(bank, lhsT=tl_[:, cs, tc * 128:(tc + 1) * 128], rhs=zt[:, tc, s_ * 512:(s_ + 1) * 512], start=(tc == 0), stop=(tc == NT - 1)),
                                         reads=[tlb_, ztb], writes=[bb], inc=(tc == NT - 1))
                            P.op("act", lambda e: e.activation(out=zr[:, 0, :], in_=bks[0][0], func=AF.Copy), reads=[bks[0][1]], writes=[zrb])
                            P.op("act", lambda e: e.activation(out=zr[:, 1, :], in_=bks[1][0], func=AF.Copy), reads=[bks[1][1]], writes=[zrb])
                            P.op("dve", lambda e: e.tensor_tensor(out=ta[:], in0=zr[:, 0, :], in1=kf_[:, 0, :], op=ALU.mult), reads=[zrb, kfb_], writes=[tab])
                            P.op("pool", lambda e: e.tensor_tensor(out=tb_[:], in0=zr[:, 1, :], in1=kf_[:, 1, :], op=ALU.mult), reads=[zrb, kfb_], writes=[tbb])
                            P.op("dve", lambda e: e.tensor_tensor(out=Y[:, fc, 0, s_ * 512:(s_ + 1) * 512], in0=ta[:], in1=tb_[:], op=ALU.subtract), reads=[tab, tbb], writes=[Yb])
                            P.op("dve", lambda e: e.tensor_tensor(out=ta[:], in0=zr[:, 0, :], in1=kf_[:, 1, :], op=ALU.mult), reads=[zrb, kfb_], writes=[tab])
                            P.op("pool", lambda e: e.tensor_tensor(out=tb_[:], in0=zr[:, 1, :], in1=kf_[:, 0, :], op=ALU.mult), reads=[zrb, kfb_], writes=[tbb])
                            P.op("dve", lambda e: e.tensor_tensor(out=Y[:, fc, 1, s_ * 512:(s_ + 1) * 512], in0=ta[:], in1=tb_[:], op=ALU.add), reads=[tab, tbb], writes=[Yb])
                    for tc in range(NT):
                        tl_, tlb_ = tsl[si_ % 2]
                        si_ += 1
                        P.dma("sp", tl_[:, 0, :], Tc[tc, :, 0:NF, :].rearrange("p a b -> p (a b)"), writes=[tlb_])
                        P.dma("sp", tl_[:, 1, :], Ts[tc, :, 0:NF, :].rearrange("p a b -> p (a b)"), writes=[tlb_])
                        for s_ in range(nseq):
                            tk = tokbase + s_ * L + tc * 128
                            xg, xgb_ = xg_[(tc * nseq + s_) % 2]
                            P.dma("sp", xg[:], X3t[1 + o_, tk:tk + 128, :], writes=[xgb_])
                            bank, bb = nb()
                            n_mm = 2 * NF
                            i_mm = 0
                            for fc in range(NF):
                                for cs in range(2):
                                    P.op("pe", lambda e: e.matmul(bank, lhsT=tl_[:, cs, fc * 128:(fc + 1) * 128], rhs=Y[:, fc, cs, s_ * 512:(s_ + 1) * 512],
                                                                  start=(i_mm == 0), stop=(i_mm == n_mm - 1)),
                                         reads=[tlb_, Yb], writes=[bb], inc=(i_mm == n_mm - 1))
                                    i_mm += 1
                            zsl = zt[:, tc, s_ * 512:(s_ + 1) * 512]
                            P.op("dve", lambda e: e.tensor_tensor(out=ga[:], in0=zsl, in1=ppt[:, od + o_ * 512:od + (o_ + 1) * 512], op=ALU.mult), reads=[ztb, ppb], writes=[gab])
                            P.op("dve", lambda e: e.tensor_tensor(out=ga[:], in0=ga[:], in1=bank, op=ALU.add), reads=[gab, bb], writes=[gab])
                            if o_ == 0:
                                P.op("dve", lambda e: e.tensor_tensor(out=zsl, in0=ga[:], in1=xg[:], op=ALU.mult), reads=[gab, xgb_], writes=[ztb])
                            else:
                                ho, hob = hyo[(tc * nseq + s_) % 2]
                                P.op("dve", lambda e: e.tensor_tensor(out=ho[:], in0=ga[:], in1=xg[:], op=ALU.mult), reads=[gab, xgb_], writes=[hob])
                                bk2, bb2 = nb()
                                bk2b = bk2.bitcast(BF16)
                                for j in range(4):
                                    P.op("pe", lambda e: e.transpose(bk2b[:, j * 128:(j + 1) * 128], ho[:, j * 128:(j + 1) * 128], idb[:]), reads=[hob, idbb], writes=[bb2], inc=(j == 3))
                                ht, htb = hts[(tc * nseq + s_) % 2]
                                P.op("act", lambda e: e.activation(out=ht[:], in_=bk2b[:, 0:512].rearrange("p (a b) -> p a b", a=4), func=AF.Copy), reads=[bb2], writes=[htb])
                                P.dma("act", mixT[1024:1536, tk:tk + 128].rearrange("(a p) t -> p a t", p=128), ht[:], reads=[htb])
                P.barrier()

        def phase_D(l, last):
            nonlocal es2
            for nm in ("out", "gate", "up", "down"):
                wait_conv(nm, l)
            with contextlib.ExitStack() as es2:
                xgs = [sb(f"xd{i}", [128, KC, G], F32) for i in range(1)]
                xn, xnb = sb("xn", [128, KC, G], BF16)
                mx, mxb = sb("mx", [128, KC, G], BF16)
                hT, hTb = sb("h2T", [128, KC, G], BF16)
                rst, rstb = sb("rstd", [128, G], F32)
                actT, actb = sb("actT", [128, HC, G], BF16)
                wsl = [sb(f"wd{i}", [128, KC, 512], BF16) for i in range(3)]
                sg = [sb(f"sg{i}", [128, G], F32) for i in range(2)]
                yst = [sb(f"yst{i}", [128, 512], F32) for i in range(2)]
                wi = 0
                gi = 0
                xgq = [Buf(f"xgq{i}") for i in range(4)]
                for g in range(NG):
                    tok0 = g * G
                    cond = 1 if g < 8 else 0
                    xg, xgb = xgs[0]
                    for cg_ in range(4):
                        P.dma("sp", xg[:, cg_ * 4:(cg_ + 1) * 4, :], xT.rearrange("(kc p) t -> p kc t", p=128)[:, cg_ * 4:(cg_ + 1) * 4, tok0:tok0 + G], writes=[xgq[cg_]])
                    P.dma("sp", mx[:], mixT.rearrange("(kc p) t -> p kc t", p=128)[:, :, tok0:tok0 + G], writes=[mxb])
                    for s in range(4):
                        wt, wtb = wsl[wi % 3]
                        wi += 1
                        P.dma("sp", wt[:], wb_out[l].rearrange("(kc p) n -> p kc n", p=128)[:, :, s * 512:(s + 1) * 512], writes=[wtb])
                        for j in range(4):
                            cc = s * 4 + j
                            bank, bb = nb()
                            for kc in range(KC):
                                P.op("pe", lambda e: e.matmul(bank, lhsT=wt[:, kc, j * 128:(j + 1) * 128], rhs=mx[:, kc, :], start=(kc == 0), stop=(kc == KC - 1)),
                                     reads=[wtb, mxb], writes=[bb], inc=(kc == KC - 1))
                            P.op("dve", lambda e: e.scalar_tensor_tensor(out=xg[:, cc, :], in0=bank, scalar=modt[:, l, 2 * 16 + cc, cond:cond + 1], in1=xg[:, cc, :],
                                                                         op0=ALU.mult, op1=ALU.add), reads=[bb, modb, xgq[cc // 4]], writes=[xgq[cc // 4]])
                    norm_mod(xg, xgq, xn, xnb, hT, hTb, hT, hTb, rst, rstb, l, 1, cond)
                    ring = wsl + [(mx, mxb)]
                    ri = 0
                    for s in range(11):
                        wg, wgb = ring[ri % 4]
                        ri += 1
                        wu, wub = ring[ri % 4]
                        ri += 1
                        P.dma("sp", wg[:], wb_gate[l].rearrange("(kc p) n -> p kc n", p=128)[:, :, s * 512:(s + 1) * 512], writes=[wgb])
                        P.dma("sp", wu[:], wb_up[l].rearrange("(kc p) n -> p kc n", p=128)[:, :, s * 512:(s + 1) * 512], writes=[wub])
                        for j in range(4):
                            hc = s * 4 + j
                            bg, bgb = nb()
                            for kc in range(KC):
                                P.op("pe", lambda e: e.matmul(bg, lhsT=wg[:, kc, j * 128:(j + 1) * 128], rhs=hT[:, kc, :], start=(kc == 0), stop=(kc == KC - 1)),
                                     reads=[wgb, hTb], writes=[bgb], inc=(kc == KC - 1))
                            bu, bub = nb()
                            for kc in range(KC):
                                P.op("pe", lambda e: e.matmul(bu, lhsT=wu[:, kc, j * 128:(j + 1) * 128], rhs=hT[:, kc, :], start=(kc == 0), stop=(kc == KC - 1)),
                                     reads=[wub, hTb], writes=[bub], inc=(kc == KC - 1))
                            sg_, sgb = sg[gi % 2]
                            gi += 1
                            P.op("act", lambda e: e.activation(out=sg_[:], in_=bg, func=AF.Silu), reads=[bgb], writes=[sgb])
                            P.op("dve", lambda e: e.tensor_tensor(out=actT[:, hc, :], in0=sg_[:], in1=bu, op=ALU.mult), reads=[sgb, bub], writes=[actb])
                    for cg in range(4):
                        banks = hold(4)
                        for hq in range(4):
                            wt, wtb = ring[ri % 4]
                            ri += 1
                            P.dma("sp", wt[:, 0:11, :], wb_down[l].rearrange("(hc p) n -> p hc n", p=128)[:, hq * 11:(hq + 1) * 11, cg * 512:(cg + 1) * 512], writes=[wtb])
                            for j in range(4):
                                for hl in range(11):
                                    fst = (hq == 0 and hl == 0)
                                    lst = (hq == 3 and hl == 10)
                                    P.op("pe", lambda e: e.matmul(banks[j][0], lhsT=wt[:, hl, j * 128:(j + 1) * 128], rhs=actT[:, hq * 11 + hl, :], start=fst, stop=lst),
                                         reads=[wtb, actb], writes=[banks[j][1]], inc=(hl == 10))
                        for j in range(4):
                            cc = cg * 4 + j
                            P.op("dve", lambda e: e.scalar_tensor_tensor(out=xg[:, cc, :], in0=banks[j][0], scalar=modt[:, l, 5 * 16 + cc, cond:cond + 1], in1=xg[:, cc, :],
                                                                         op0=ALU.mult, op1=ALU.add), reads=[banks[j][1], modb, xgq[cg]], writes=[xgq[cg]])
                        release(banks)
                        if not last:
                            P.dma("act", xT.rearrange("(kc p) t -> p kc t", p=128)[:, cg * 4:(cg + 1) * 4, tok0:tok0 + G], xg[:, cg * 4:(cg + 1) * 4, :], reads=[xgq[cg]])
                    if not last:
                        pass
                    else:
                        for tt in range(4):
                            for k4 in range(4):
                                ys_, ysb = yst[k4 % 2]
                                bank, bb = nb()
                                for j in range(4):
                                    kc = k4 * 4 + j
                                    P.op("pe", lambda e: e.transpose(bank[:, j * 128:(j + 1) * 128], xg[:, kc, tt * 128:(tt + 1) * 128], idf[:]), reads=[xgq[k4], idfb], writes=[bb], inc=(j == 3))
                                if k4 % 2 == 0:
                                    P.op("act", lambda e: e.activation(out=ys_[:], in_=bank, func=AF.Copy), reads=[bb], writes=[ysb])
                                else:
                                    P.op("dve", lambda e: e.tensor_copy(out=ys_[:], in_=bank), reads=[bb], writes=[ysb])
                                tok = tok0 + tt * 128
                                dst = ys[tok:tok + 128, k4 * 512:(k4 + 1) * 512] if tok < LS else yp[tok - LS:tok - LS + 128, k4 * 512:(k4 + 1) * 512]
                                P.dma("act", dst, ys_[:], reads=[ysb])
                P.barrier()

        phase_mod(0)
        if stop_after == ("mod", 0):
            stages = []
        phase_xT()
        stages = []
        if stop_after not in (("mod", 0), ("xT", 0)):
            for l in range(nlayers):
                stages += [("W", l), ("A", l), ("B", l), ("C1", l), ("C2s", l), ("C2p", l), ("D", l)]
        for (st, l) in stages:
            if st == "W":
                if l == 0:
                    phase_W(l, ["in"])
            elif st == "A":
                phase_A(l)
                if l + 1 < nlayers:
                    phase_mod(l + 1)
            elif st == "B":
                phase_B(l)
            elif st == "C1":
                phase_C1(l)
            elif st == "C2s":
                phase_C2(l, LS, 1, 0, "tcs", "tss", "gns", "zfs", "ntls", Kfs)
            elif st == "C2p":
                phase_C2(l, LP, NPS, LS, "tcp", "tsp", "gnp", "zfp", "ntlp", Kfp)
            elif st == "D":
                phase_D(l, last=(l == nlayers - 1))
            if stop_after == (st, l):
                break
        P.barrier()
    return nc


_W_KEYS = ("w_ada", "w_in", "w_out", "w_gate", "w_up", "w_down", "hy_w1", "hy_w2", "hy_w3", "w_pool")


def make_in_maps(inp, cores):
    cst = _constants()
    maps = []
    wts = {k: np.ascontiguousarray(np.asarray(inp[k], np.float32)) for k in _W_KEYS}
    for i in cores:
        m = {}
        m["xs"] = np.ascontiguousarray(np.asarray(inp["x_sample"][i], np.float32))
        m["xp"] = np.ascontiguousarray(np.asarray(inp["x_prompt"][4 * i:4 * i + 4], np.float32).reshape(NPS * LP, D))
        m["ck"] = np.ascontiguousarray(np.asarray(inp["cache_k"][i], np.float32).reshape(DEPTH, LP, 1024))
        m["cv"] = np.ascontiguousarray(np.asarray(inp["cache_v"][i], np.float32).reshape(DEPTH, LP, 1024))
        m["pp"] = _pack_params(inp, i)
        m.update(wts)
        for k, v in cst.items():
            m["c_" + k] = v
        maps.append(m)
    return maps


def kernel(**inputs):
    inp = {k: np.asarray(v) for k, v in inputs.items()}
    nc = build_program()
    maps = make_in_maps(inp, list(range(NCORES)))
    res = run_bass_kernel_spmd(nc, maps, core_ids=list(range(NCORES)))
    r = res.results
    y_s = np.stack([np.asarray(r[i]["ys"], np.float32) for i in range(NCORES)], 0)
    y_p = np.concatenate([np.asarray(r[i]["yp"], np.float32).reshape(NPS, LP, D) for i in range(NCORES)], 0)
    s_k = np.concatenate([np.asarray(r[i]["sk"], np.float32).reshape(NPS, DEPTH, LP, H, 2, 64) for i in range(NCORES)], 0)
    s_v = np.concatenate([np.asarray(r[i]["sv"], np.float32).reshape(NPS, DEPTH, LP, H, 128) for i in range(NCORES)], 0)
    return (y_p, y_s, s_k, s_v)
```
